# Optimizing a Trainium2 kernel written in Bass

```python
import math
import jax, jax.numpy as jnp
from jax import lax
import numpy as np

D_MODEL = 1024
BATCH = 8
SEQ = 2048
DEPTH = 2
DEC_BATCH = 16
DEC_SEQ = 64
PAST_LEN = 2048

CHUNK = 64
D_MIX = D_MODEL
D_ATTN = D_MIX // 2
D_CONV = D_MIX - D_ATTN
N_DIFF_HEADS = 4
N_SUB_HEADS = 2 * N_DIFF_HEADS
HEAD_DIM = D_ATTN // N_SUB_HEADS
V_DIM = 2 * HEAD_DIM
CONV_WIDTH = 31
CONV_HIST = CONV_WIDTH - 1
D_FF = 4 * D_MODEL
D_IN_PROJ = 3 * D_ATTN + 2 * D_CONV
ROPE_THETA = 10000.0
Q_BLOCK = 128
LN_EPS = 1e-5
RMS_EPS = 1e-5
DEEPNORM_ALPHA = (2.0 * DEPTH) ** 0.25
DEEPNORM_BETA = (8.0 * DEPTH) ** -0.25

kernel_name = "hymba_diffattn_conformer_stream_step"


def lambda_init(layer):
    return 0.8 - 0.6 * math.exp(-0.3 * layer)


def layer_norm(x, g, b):
    xf = x.astype(jnp.float32)
    mu = jnp.mean(xf, axis=-1, keepdims=True)
    var = jnp.mean(jnp.square(xf - mu), axis=-1, keepdims=True)
    return ((xf - mu) * lax.rsqrt(var + LN_EPS) * g + b).astype(x.dtype)


def rope(x, pos):
    half = HEAD_DIM // 2
    inv = ROPE_THETA ** (-jnp.arange(half, dtype=jnp.float32) / half)
    ang = pos.astype(jnp.float32)[:, None] * inv[None, :]
    cos = jnp.cos(ang)[None, :, None, :]
    sin = jnp.sin(ang)[None, :, None, :]
    xf = x.astype(jnp.float32)
    x1, x2 = xf[..., :half], xf[..., half:]
    return jnp.concatenate([x1 * cos - x2 * sin, x2 * cos + x1 * sin], axis=-1).astype(x.dtype)


def diff_core(q, k, v, mask, lam, sub_g, lam_init):
    B, Tq = q.shape[0], q.shape[1]
    Tk = k.shape[1]
    s = jnp.einsum('bqsd,bksd->bsqk', q, k).astype(jnp.float32) * (HEAD_DIM ** -0.5)
    if mask is not None:
        s = jnp.where(mask[None, None], s, -1e30)
    p = jax.nn.softmax(s, axis=-1).reshape(B, N_DIFF_HEADS, 2, Tq, Tk)
    a = p[:, :, 0] - lam * p[:, :, 1]
    o = jnp.einsum('bhqk,bkhe->bqhe', a, v.astype(jnp.float32))
    o = o * lax.rsqrt(jnp.mean(jnp.square(o), axis=-1, keepdims=True) + RMS_EPS)
    return (o * sub_g * (1.0 - lam_init)).astype(q.dtype)


def prompt_diff_attention(q, k, v, lam, sub_g, lam_init):
    B, T = q.shape[0], q.shape[1]
    nb = T // Q_BLOCK
    qb = q.reshape(B, nb, Q_BLOCK, N_SUB_HEADS, HEAD_DIM).transpose(1, 0, 2, 3, 4)
    key_chunk = jnp.arange(T) // CHUNK

    def one_block(args):
        q_i, i = args
        q_chunk = (i * Q_BLOCK + jnp.arange(Q_BLOCK)) // CHUNK
        mask = key_chunk[None, :] <= q_chunk[:, None]
        return diff_core(q_i, k, v, mask, lam, sub_g, lam_init)

    out = lax.map(one_block, (qb, jnp.arange(nb)))
    return out.transpose(1, 0, 2, 3, 4).reshape(B, T, N_DIFF_HEADS, V_DIM)


def causal_depthwise_conv(u, hist, w, b):
    full = jnp.concatenate([hist, u], axis=1)
    y = lax.conv_general_dilated(full, w[:, None, :], window_strides=(1,), padding='VALID',
                                 dimension_numbers=('NWC', 'WIO', 'NWC'),
                                 feature_group_count=D_CONV)
    return y + b, full[:, -CONV_HIST:]


def trunk_layer(l, x, pos, past_k, past_v, past_conv, p):
    B, T, _ = x.shape
    lam_init = lambda_init(l)
    proj = jnp.einsum('btd,de->bte', x, p['w_in'][l])
    q, k, v, ca, cg = jnp.split(proj, [D_ATTN, 2 * D_ATTN, 3 * D_ATTN, 3 * D_ATTN + D_CONV], axis=-1)
    q = rope(q.reshape(B, T, N_SUB_HEADS, HEAD_DIM), pos)
    k = rope(k.reshape(B, T, N_SUB_HEADS, HEAD_DIM), pos)
    v = v.reshape(B, T, N_DIFF_HEADS, V_DIM)
    lam = (jnp.exp(jnp.sum(p['lambda_q1'][l].astype(jnp.float32) * p['lambda_k1'][l].astype(jnp.float32)))
           - jnp.exp(jnp.sum(p['lambda_q2'][l].astype(jnp.float32) * p['lambda_k2'][l].astype(jnp.float32)))
           + lam_init)
    sub_g = p['subln_g'][l].astype(jnp.float32)
    if past_k is None:
        attn = prompt_diff_attention(q, k, v, lam, sub_g, lam_init)
        hist = jnp.zeros((B, CONV_HIST, D_CONV), x.dtype)
    else:
        k_all = jnp.concatenate([past_k, k], axis=1)
        v_all = jnp.concatenate([past_v, v], axis=1)
        attn = diff_core(q, k_all, v_all, None, lam, sub_g, lam_init)
        hist = past_conv
    u = ca * jax.nn.sigmoid(cg)
    c, conv_state = causal_depthwise_conv(u, hist, p['conv_w'][l], p['conv_b'][l])
    c = jax.nn.silu(layer_norm(c, p['conv_ln_g'][l], p['conv_ln_b'][l]))
    mix = jnp.concatenate([attn.reshape(B, T, D_ATTN), c], axis=-1)
    y = jnp.einsum('bte,ed->btd', mix, p['w_out'][l])
    x = layer_norm(DEEPNORM_ALPHA * x + y, p['ln1_g'][l], p['ln1_b'][l])
    h = jnp.square(jax.nn.relu(jnp.einsum('btd,df->btf', x, p['w_ff1'][l])))
    x = layer_norm(DEEPNORM_ALPHA * x + jnp.einsum('btf,fd->btd', h, p['w_ff2'][l]), p['ln2_g'][l], p['ln2_b'][l])
    return x, k, v, conv_state


def setup_inputs(seed: int = 0) -> dict:
    key = jax.random.key(seed)
    ks = jax.random.split(key, 24)
    f32 = jnp.float32
    nrm = lambda k, shape, s: jax.random.normal(k, shape, f32) * s
    w_in = nrm(ks[0], (DEPTH, D_MODEL, D_IN_PROJ), D_MODEL ** -0.5)
    w_in = w_in.at[..., 2 * D_ATTN:3 * D_ATTN].multiply(DEEPNORM_BETA)
    return {
        'x_prompt': nrm(ks[1], (BATCH, SEQ, D_MODEL), 1.0),
        'x_sample': nrm(ks[2], (DEC_BATCH, DEC_SEQ, D_MODEL), 1.0),
        'cache_k': nrm(ks[3], (DEPTH, DEC_BATCH, PAST_LEN, N_SUB_HEADS, HEAD_DIM), 1.0),
        'cache_v': nrm(ks[4], (DEPTH, DEC_BATCH, PAST_LEN, N_DIFF_HEADS, V_DIM), DEEPNORM_BETA),
        'cache_conv': nrm(ks[5], (DEPTH, DEC_BATCH, CONV_HIST, D_CONV), 0.5),
        'w_in': w_in,
        'lambda_q1': nrm(ks[6], (DEPTH, HEAD_DIM), 0.1),
        'lambda_k1': nrm(ks[7], (DEPTH, HEAD_DIM), 0.1),
        'lambda_q2': nrm(ks[8], (DEPTH, HEAD_DIM), 0.1),
        'lambda_k2': nrm(ks[9], (DEPTH, HEAD_DIM), 0.1),
        'subln_g': 1.0 + nrm(ks[10], (DEPTH, V_DIM), 0.02),
        'conv_w': nrm(ks[11], (DEPTH, CONV_WIDTH, D_CONV), CONV_WIDTH ** -0.5),
        'conv_b': nrm(ks[12], (DEPTH, D_CONV), 0.02),
        'conv_ln_g': 1.0 + nrm(ks[13], (DEPTH, D_CONV), 0.02),
        'conv_ln_b': nrm(ks[14], (DEPTH, D_CONV), 0.02),
        'w_out': nrm(ks[15], (DEPTH, D_MIX, D_MODEL), DEEPNORM_BETA * D_MIX ** -0.5),
        'ln1_g': 1.0 + nrm(ks[16], (DEPTH, D_MODEL), 0.02),
        'ln1_b': nrm(ks[17], (DEPTH, D_MODEL), 0.02),
        'w_ff1': nrm(ks[18], (DEPTH, D_MODEL, D_FF), DEEPNORM_BETA * D_MODEL ** -0.5),
        'w_ff2': nrm(ks[19], (DEPTH, D_FF, D_MODEL), DEEPNORM_BETA * D_FF ** -0.5),
        'ln2_g': 1.0 + nrm(ks[20], (DEPTH, D_MODEL), 0.02),
        'ln2_b': nrm(ks[21], (DEPTH, D_MODEL), 0.02),
    }


def reference(x_prompt, x_sample, cache_k, cache_v, cache_conv, w_in, lambda_q1, lambda_k1,
              lambda_q2, lambda_k2, subln_g, conv_w, conv_b, conv_ln_g, conv_ln_b, w_out,
              ln1_g, ln1_b, w_ff1, w_ff2, ln2_g, ln2_b):
    p = {'w_in': w_in, 'lambda_q1': lambda_q1, 'lambda_k1': lambda_k1, 'lambda_q2': lambda_q2,
         'lambda_k2': lambda_k2, 'subln_g': subln_g, 'conv_w': conv_w, 'conv_b': conv_b,
         'conv_ln_g': conv_ln_g, 'conv_ln_b': conv_ln_b, 'w_out': w_out, 'ln1_g': ln1_g,
         'ln1_b': ln1_b, 'w_ff1': w_ff1, 'w_ff2': w_ff2, 'ln2_g': ln2_g, 'ln2_b': ln2_b}
    T_p = x_prompt.shape[1]
    T_s = x_sample.shape[1]
    P = cache_k.shape[2]
    pos_p = jnp.arange(T_p)
    pos_s = P + jnp.arange(T_s)
    xp, xs = x_prompt, x_sample
    kp, vp, cp, ksl, vsl, csl = [], [], [], [], [], []
    for l in range(DEPTH):
        xp, k_new, v_new, c_new = trunk_layer(l, xp, pos_p, None, None, None, p)
        kp.append(k_new); vp.append(v_new); cp.append(c_new)
        xs, k_new, v_new, c_new = trunk_layer(l, xs, pos_s, cache_k[l], cache_v[l], cache_conv[l], p)
        ksl.append(k_new); vsl.append(v_new); csl.append(c_new)
    new_k_prompt = jnp.stack(kp)
    new_v_prompt = jnp.stack(vp)
    new_conv_prompt = jnp.stack(cp)
    new_k_sample = jnp.stack(ksl)
    new_v_sample = jnp.stack(vsl)
    new_conv_sample = jnp.stack(csl)
    return (xp, xs, new_k_prompt, new_v_prompt, new_conv_prompt, new_k_sample, new_v_sample, new_conv_sample)
```

```python
import math
import numpy as np
import concourse.bass as bass
import concourse.mybir as mybir
from concourse.bass_utils import run_bass_kernel_spmd

F32 = mybir.dt.float32
BF16 = mybir.dt.bfloat16
AF = mybir.ActivationFunctionType
ALU = mybir.AluOpType
AX = mybir.AxisListType

D = 1024
DEPTH = 2
NCORE = 8
TP = 2048
TS = 64
TOK = TP + 2 * TS
NT = 18
DIN = 2560
DFF = 4096
CW = 31
HIST = 30
ALPHA = (2.0 * DEPTH) ** 0.25
LN_EPS = 1e-5
RMS_EPS = 1e-5
STOP_AFTER = None
DEBUG_CORES = None
DEBUG_DUMP = False
DEBUG_OUT = {}


def lambda_init(layer):
    return 0.8 - 0.6 * math.exp(-0.3 * layer)


_GEOM = {"merged": True}
NMT = 17


def trows(t):
    if _GEOM["merged"] and t == 16:
        return 128
    return 128 if t < 16 else 64


def tcol(t):
    return t * 128 if t < 16 else TP + (t - 16) * TS


def cover(t):
    return [16, 17] if (_GEOM["merged"] and t == 16) else [t]


BLOCKS = [(0, 512), (512, 512), (1024, 512), (1536, 512), (2048, 128)]


def block_tiles(bi):
    return [4 * bi + i for i in range(4)] if bi < 4 else [16, 17]


class Buf:
    __slots__ = ("name", "lw", "rd", "wsem", "wcnt", "rsem", "rcnt", "excl")

    def __init__(self, name):
        self.name = name
        self.excl = False
        self.lw = None
        self.rd = []
        self.wsem = None
        self.wcnt = 0
        self.rsem = None
        self.rcnt = 0


class Eng:
    def __init__(self, name, sem):
        self.name = name
        self.sem = sem
        self.n = 0
        self.ops = []
        self.waited = {}


class FW:
    def __init__(self, nc):
        self.nc = nc
        self._ctx = []
        self.engs = {}
        for nm in ("pe", "act", "dve", "pool", "sp"):
            self.engs[nm] = Eng(nm, self.new_sem("e_" + nm))
        self.all_bufs = []

    def new_sem(self, name):
        self._nsem = getattr(self, "_nsem", 0) + 1
        cm = self.nc.semaphore("%s_%d" % (name, self._nsem))
        s = cm.__enter__()
        self._ctx.append(cm)
        return s

    def buf(self, name):
        b = Buf(name)
        self.all_bufs.append(b)
        return b

    def bufs(self, name, n):
        return [self.buf(f"{name}{i}") for i in range(n)]

    def _deps(self, reads, writes, esem=None):
        deps = []
        for b in reads:
            if b.lw is not None:
                deps.append(b.lw)
            if b.excl:
                deps.extend(r for r in b.rd if r[0] is not esem)
        for b in writes:
            if b.lw is not None:
                deps.append(b.lw)
            deps.extend(b.rd)
        return deps

    def _filter(self, e, deps, skip_self=False):
        best = {}
        semobj = {}
        for (s, v) in deps:
            k = id(s)
            if v > best.get(k, 0):
                best[k] = v
                semobj[k] = s
        waits = []
        for k, v in best.items():
            if skip_self and semobj[k] is e.sem:
                continue
            if e.waited.get(k, 0) >= v:
                continue
            e.waited[k] = v
            waits.append((semobj[k], v))
        return waits

    def op(self, eng, fn, reads=(), writes=()):
        e = self.engs[eng]
        waits = self._filter(e, self._deps(reads, writes, e.sem), skip_self=(eng == "pe"))
        e.n += 1
        tok = (e.sem, e.n)
        e.ops.append((waits, fn, 1))
        for b in reads:
            b.rd.append(tok)
        for b in writes:
            b.lw = tok
            b.rd = []
        return tok

    def dma(self, q, fn, owner, kind, reads=(), writes=(), n=1):
        e = self.engs[q]
        waits = self._filter(e, self._deps(reads, writes))
        if kind == "w":
            if owner.wsem is None:
                owner.wsem = self.new_sem("dw_" + owner.name)
            sem = owner.wsem
            owner.wcnt += n
            tok = (sem, 16 * owner.wcnt)
        else:
            if owner.rsem is None:
                owner.rsem = self.new_sem("dr_" + owner.name)
            sem = owner.rsem
            owner.rcnt += n
            tok = (sem, 16 * owner.rcnt)
        e.ops.append((waits, (lambda h, fn=fn, sem=sem: fn(h, sem)), 0))
        for b in reads:
            b.rd.append(tok)
        for b in writes:
            b.lw = tok
            b.rd = []
        return tok

    def barrier(self):
        deps = []
        for e in self.engs.values():
            if e.n > 0:
                deps.append((e.sem, e.n))
        for b in self.all_bufs:
            if b.wsem is not None and b.wcnt:
                deps.append((b.wsem, 16 * b.wcnt))
            if b.rsem is not None and b.rcnt:
                deps.append((b.rsem, 16 * b.rcnt))
        for e in self.engs.values():
            waits = self._filter(e, deps)
            if waits:
                e.ops.append((waits, None, 0))

    def emit(self):
        nc = self.nc
        with nc.Block() as block:
            def run(e):
                def body(h):
                    for (waits, fn, inc) in e.ops:
                        for (s, v) in waits:
                            h.wait_ge(s, v)
                        if fn is None:
                            continue
                        ins = fn(h)
                        if inc:
                            ins.then_inc(e.sem, 1)
                return body
            block.tensor(run(self.engs["pe"]))
            block.scalar(run(self.engs["act"]))
            block.vector(run(self.engs["dve"]))
            block.gpsimd(run(self.engs["pool"]))
            block.sync(run(self.engs["sp"]))

    def close(self):
        for cm in reversed(self._ctx):
            cm.__exit__(None, None, None)
        self._ctx = []


def build_program():
    nc = bass.Bass("TRN2", target_bir_lowering=False)
    fw = FW(nc)

    def din(name, shape):
        return nc.dram_tensor(name, list(shape), F32, kind="ExternalInput").ap()

    def dout(name, shape):
        return nc.dram_tensor(name, list(shape), F32, kind="ExternalOutput").ap()

    x_d = din("x", [TOK, D])
    ck_d = din("ck", [DEPTH, 2, TP, 512])
    cv_d = din("cv", [DEPTH, 2, TP, 512])
    cc_d = din("cc", [DEPTH, 2, HIST, 512])
    win_d = din("w_in", [DEPTH, D, DIN])
    wout_d = din("w_out", [DEPTH, D, D])
    w1_d = din("w_ff1", [DEPTH, D, DFF])
    w2_d = din("w_ff2", [DEPTH, DFF, D])
    lq1_d = din("lq1", [DEPTH, 64]); lk1_d = din("lk1", [DEPTH, 64])
    lq2_d = din("lq2", [DEPTH, 64]); lk2_d = din("lk2", [DEPTH, 64])
    subg_d = din("subln_g", [DEPTH, 128])
    convw_d = din("conv_w", [DEPTH, CW, 512])
    convb_d = din("conv_b", [DEPTH, 512])
    clng_d = din("conv_ln_g", [DEPTH, 512]); clnb_d = din("conv_ln_b", [DEPTH, 512])
    ln1g_d = din("ln1_g", [DEPTH, D]); ln1b_d = din("ln1_b", [DEPTH, D])
    ln2g_d = din("ln2_g", [DEPTH, D]); ln2b_d = din("ln2_b", [DEPTH, D])
    cos_d = din("rope_cos", [128, NT, 32]); sin_d = din("rope_sin", [128, NT, 32])

    y_d = dout("y", [TOK, D])
    nk_d = dout("nk", [DEPTH, TOK, 512])
    nv_d = dout("nv", [DEPTH, TOK, 512])
    ncv_d = dout("ncv", [DEPTH, 3, HIST, 512])
    xsc_d = nc.dram_tensor("xsc", [TOK, D], F32, kind="Internal").ap()

    SB0 = 16512
    SBTOP = 229344
    cur = [SB0]
    lim = [SBTOP]

    def alloc(name, shape, dt, at=None):
        nbytes = int(np.prod(shape[1:])) * (4 if dt == F32 else 2)
        nbytes = (nbytes + 31) // 32 * 32
        if at is None:
            off = cur[0]
            cur[0] += nbytes
            assert cur[0] <= lim[0], (name, cur[0], lim[0])
        else:
            off = at
        return nc.alloc_sbuf_tensor_at(name, list(shape), dt, offset=off), off, nbytes

    actT, _, _ = alloc("actT", [128, 8, TOK], BF16)
    big2, _, _ = alloc("big2", [128, 8, TOK], BF16)
    wring = []
    for i in range(4):
        wt, _, _ = alloc(f"wring{i}", [128, 8, 512], BF16)
        wring.append(wt)
    identb, _, _ = alloc("identb", [128, 128], BF16)
    identf, _, _ = alloc("identf", [128, 128], F32)
    onesb, _, _ = alloc("onesb", [128, 128], BF16)
    ones512, _, _ = alloc("ones512", [128, 128], F32)
    ones128, _, _ = alloc("ones128", [128, 128], F32)
    epsln, _, _ = alloc("epsln", [128, 1], F32)
    lamt, _, _ = alloc("lamt", [128, 4, 64], F32)
    lamw, _, _ = alloc("lamw", [128, 8], F32)
    subg, _, _ = alloc("subg", [128, 2], F32)
    cvec, _, _ = alloc("cvec", [128, 3, 4], F32)
    convwT, _, _ = alloc("convwT", [128, 4, 32], F32)
    xin = []
    for i in range(2):
        t_, _, _ = alloc(f"xin{i}", [128, D], F32)
        xin.append(t_)
    xb16 = []
    for i in range(2):
        t_, _, _ = alloc(f"xb16_{i}", [128, D], BF16)
        xb16.append(t_)
    XOFF = cur[0]
    XSIZE = NT * D * 4
    cur[0] += XSIZE
    assert cur[0] <= SBTOP
    MOFF = cur[0]
    xres = nc.alloc_sbuf_tensor_at("xres", [128, NT, D], F32, offset=XOFF)
    o = XOFF
    vbf = nc.alloc_sbuf_tensor_at("vbf", [128, NT, 512], BF16, offset=o); o += NT * 512 * 2
    u_p = nc.alloc_sbuf_tensor_at("u_p", [128, 4, HIST + TP], BF16, offset=o); o += 4 * (HIST + TP) * 2 + 16
    o = (o + 31) // 32 * 32
    u_s = nc.alloc_sbuf_tensor_at("u_s", [128, 2, 4, HIST + TS + 2], BF16, offset=o); o += 2 * 4 * (HIST + TS + 2) * 2
    o = (o + 31) // 32 * 32
    XB_FREE = o
    assert XB_FREE + 31744 <= XOFF + XSIZE, (XB_FREE, XOFF + XSIZE)
    diag = nc.alloc_sbuf_tensor_at("diag", [128, 4, CW, 128], BF16, offset=XB_FREE)
    MSIZE = SBTOP - MOFF
    print("SBUF map: XOFF", XOFF, "MOFF", MOFF, "MSIZE", MSIZE)

    def malloc_reset(where="M"):
        if where == "M":
            cur[0] = MOFF; lim[0] = SBTOP
        elif where == "XU":
            cur[0] = XOFF + NT * 512 * 2; lim[0] = XOFF + XSIZE
        else:
            cur[0] = XB_FREE; lim[0] = XOFF + XSIZE

    psb = [nc.alloc_psum_tensor(f"ps{i}", [128, 512], F32) for i in range(8)]
    PB = fw.bufs("psum", 8)
    for b_ in PB:
        b_.excl = True
    rr = [0]

    def next_bank():
        i = rr[0] % 8
        rr[0] += 1
        return i

    actT_b = [[fw.buf(f"actT_{c}_{t}") for t in range(NT)] for c in range(8)]
    big2_b = [[fw.buf(f"big2_{c}_{t}") for t in range(NT)] for c in range(8)]
    wring_b = fw.bufs("wring", 4)
    const_b = fw.buf("const")
    par_b = fw.buf("params")
    xin_b = fw.bufs("xin", 2)
    xb16_b = fw.bufs("xb16", 2)
    xres_b = fw.bufs("xres", NT)
    xsc_b = fw.bufs("xsc", NT)
    vbf_b = fw.bufs("vbf", NT)
    u_b = [fw.bufs(f"u{j}_", 7) for j in range(4)]
    diag_b = fw.bufs("diag", 4)

    def cols_tiles(c0, n):
        res = []
        for t in range(NT):
            a = tcol(t)
            if a < c0 + n and a + trows(t) > c0:
                res.append(t)
        return res

    def k_const():
        fw.op("pool", lambda h: h.memset(identf[:], 0.0), writes=[const_b])
        fw.op("pool", lambda h: h.affine_select(out=identf[:], in_=identf[:], pattern=[[-1, 128]],
                                                 compare_op=ALU.not_equal, fill=1.0, base=0, channel_multiplier=1),
              reads=[const_b], writes=[const_b])
        fw.op("dve", lambda h: h.tensor_copy(identb[:], identf[:]), reads=[const_b], writes=[const_b])
        fw.op("dve", lambda h: h.memset(onesb[:], 1.0), writes=[const_b])
        fw.op("dve", lambda h: h.memset(ones512[:], 1.0 / 512.0), writes=[const_b])
        fw.op("dve", lambda h: h.memset(ones128[:], 1.0 / 128.0), writes=[const_b])
        fw.op("dve", lambda h: h.memset(epsln[:], LN_EPS), writes=[const_b])

    pieces = []
    for l in range(DEPTH):
        wi = win_d[l].rearrange("(kc p) n -> p kc n", p=128)
        pieces.append([(wi[:, :, 0:512], (slice(0, 8), slice(0, 512)))])
        pieces.append([(wi[:, :, 512:1024], (slice(0, 8), slice(0, 512)))])
        pieces.append([(wi[:, :, 1024:1536], (slice(0, 8), slice(0, 512)))])
        for half in range(2):
            pieces.append([(wi[:, :, 1536 + 256 * half:1536 + 256 * half + 256], (slice(0, 8), slice(0, 256))),
                           (wi[:, :, 2048 + 256 * half:2048 + 256 * half + 256], (slice(0, 8), slice(256, 512)))])
        wo = wout_d[l].rearrange("(kc p) n -> p kc n", p=128)
        pieces.append([(wo[:, :, 0:512], (slice(0, 8), slice(0, 512)))])
        pieces.append([(wo[:, :, 512:1024], (slice(0, 8), slice(0, 512)))])
        w1 = w1_d[l].rearrange("(kc p) n -> p kc n", p=128)
        w2 = w2_d[l].rearrange("(fc p) n -> p fc n", p=128)
        for g in range(8):
            pieces.append([(w1[:, :, g * 512:(g + 1) * 512], (slice(0, 8), slice(0, 512)))])
            pieces.append([(w2[:, 4 * g:4 * g + 4, :], "w2")])
    pstate = {"loaded": 0, "used": 0}

    def _load_next_piece():
        i = pstate["loaded"]
        if i >= len(pieces):
            return
        slot = i % 4
        srcs = pieces[i]
        wt = wring[slot]

        def fn(h, s, srcs=srcs, wt=wt):
            ins = None
            for (src, dst) in srcs:
                if dst == "w2":
                    o_ap = wt[:].rearrange("p a b -> p (a b)").rearrange("p (f n) -> p f n", f=4)
                    ins = h.dma_start(out=o_ap, in_=src).then_inc(s, 16)
                else:
                    ins = h.dma_start(out=wt[:, dst[0], dst[1]], in_=src).then_inc(s, 16)
            return ins
        extra = [xin_b[0], xin_b[1]] if (1 <= i <= 3) else []
        fw.dma("pool", fn, wring_b[slot], "w", reads=extra, writes=[wring_b[slot]], n=len(srcs))
        pstate["loaded"] += 1

    def acquire_piece():
        while pstate["loaded"] <= pstate["used"]:
            _load_next_piece()
        i = pstate["used"]
        pstate["used"] += 1
        return i % 4

    def prefetch_pieces(nheld=1):
        while pstate["loaded"] < min(len(pieces), pstate["used"] - nheld + 4):
            _load_next_piece()

    def make_xT_front(t, src_ap, src_bufs, par):
        rows = trows(t)
        xb = xb16[par]
        fw.op("act", lambda h: h.copy(xb[:rows, :], src_ap), reads=src_bufs, writes=[xb16_b[par]])

    def make_xT_back(t, par, evac="act"):
        rows = trows(t)
        c0 = tcol(t)
        xb = xb16[par]
        bi = next_bank()
        pst = psb[bi][:].bitcast(BF16)

        def tr(h):
            ins = None
            for c in range(8):
                ins = h.transpose(pst[:, c * 128:c * 128 + rows], xb[:rows, c * 128:(c + 1) * 128], identb[:rows, :rows])
            return ins
        fw.op("pe", tr, reads=[xb16_b[par], const_b], writes=[PB[bi]])
        src = pst.rearrange("p (c r) -> p c r", c=8)[:, :, :rows]
        if evac == "act":
            fw.op("act", lambda h: h.copy(actT[:, :, c0:c0 + rows], src),
                  reads=[PB[bi]], writes=[actT_b[c][tt] for c in range(8) for tt in cover(t)])
        else:
            fw.op("dve", lambda h: h.tensor_copy(actT[:, :, c0:c0 + rows], src),
                  reads=[PB[bi]], writes=[actT_b[c][tt] for c in range(8) for tt in cover(t)])

    def make_xT(t, src_ap, src_bufs, par):
        make_xT_front(t, src_ap, src_bufs, par)
        make_xT_back(t, par, evac="dve")

    def run_multi(items):
        n = len(items)
        K = max(len(it) for it in items)
        for step in range(n + K - 1):
            for k in range(K):
                i = step - k
                if 0 <= i < n and k < len(items[i]):
                    items[i][k]()

    def run_stages(stages, depth):
        n = len(stages)
        for i in range(n + depth):
            if i < n:
                stages[i][0]()
            if i >= depth:
                stages[i - depth][1]()

    def load_params(l):
        def fn(h, s):
            h.dma_start(out=lamt[:, 0, :], in_=lq1_d[l].partition_broadcast(128)).then_inc(s, 16)
            h.dma_start(out=lamt[:, 1, :], in_=lk1_d[l].partition_broadcast(128)).then_inc(s, 16)
            h.dma_start(out=lamt[:, 2, :], in_=lq2_d[l].partition_broadcast(128)).then_inc(s, 16)
            h.dma_start(out=lamt[:, 3, :], in_=lk2_d[l].partition_broadcast(128)).then_inc(s, 16)
            h.dma_start(out=subg[:, 0:1], in_=subg_d[l].rearrange("(p o) -> p o", o=1)).then_inc(s, 16)
            with nc.allow_non_contiguous_dma(reason="tiny per-channel vectors"):
                h.dma_start(out=cvec[:, 0, :], in_=convb_d[l].rearrange("(j p) -> p j", p=128)).then_inc(s, 16)
                h.dma_start(out=cvec[:, 1, :], in_=clng_d[l].rearrange("(j p) -> p j", p=128)).then_inc(s, 16)
                ins = h.dma_start(out=cvec[:, 2, :], in_=clnb_d[l].rearrange("(j p) -> p j", p=128)).then_inc(s, 16)
            return ins
        fw.dma("sp", fn, par_b, "w", writes=[par_b], n=8)
        fw.op("dve", lambda h: h.tensor_tensor(lamt[:, 0, :], lamt[:, 0, :], lamt[:, 1, :], ALU.mult), reads=[par_b], writes=[par_b])
        fw.op("dve", lambda h: h.tensor_tensor(lamt[:, 2, :], lamt[:, 2, :], lamt[:, 3, :], ALU.mult), reads=[par_b], writes=[par_b])
        fw.op("dve", lambda h: h.reduce_sum(lamw[:, 0:1], lamt[:, 0, :], AX.X), reads=[par_b], writes=[par_b])
        fw.op("dve", lambda h: h.reduce_sum(lamw[:, 1:2], lamt[:, 2, :], AX.X), reads=[par_b], writes=[par_b])
        fw.op("act", lambda h: h.activation(lamw[:, 2:4], lamw[:, 0:2], AF.Exp), reads=[par_b], writes=[par_b])
        fw.op("dve", lambda h: h.tensor_tensor(lamw[:, 4:5], lamw[:, 3:4], lamw[:, 2:3], ALU.subtract), reads=[par_b], writes=[par_b])
        fw.op("dve", lambda h: h.tensor_scalar(lamw[:, 5:6], lamw[:, 4:5], -lambda_init(l), None, ALU.add), reads=[par_b], writes=[par_b])
        fw.op("dve", lambda h: h.tensor_scalar(subg[:, 1:2], subg[:, 0:1], 1.0 - lambda_init(l), None, ALU.mult), reads=[par_b], writes=[par_b])

    def phase_A(l):
        malloc_reset("M")
        cosT, _, _ = alloc("cosT%d" % l, [128, NT, 32], F32)
        sinT, _, _ = alloc("sinT%d" % l, [128, NT, 32], F32)
        malloc_reset("X")
        rope_b = gbuf("rope")
        t1 = [alloc(f"t1_{l}_{i}", [128, 512], F32)[0] for i in range(2)]
        t2 = [alloc(f"t2_{l}_{i}", [128, 512], F32)[0] for i in range(2)]
        t1_b = gbufs("t1_", 2); t2_b = gbufs("t2_", 2)
        kf = [alloc(f"kf_{l}_{i}", [128, 512], F32)[0] for i in range(3)]
        kf_b = gbufs("kf", 3)
        vf = [alloc(f"vf_{l}_{i}", [128, 512], F32)[0] for i in range(2)]
        vf_b = gbufs("vf", 2)
        qb = [alloc(f"qb_{l}_{i}", [128, 512], BF16)[0] for i in range(3)]
        qb_b = gbufs("qb", 3)
        sig = [alloc(f"sig_{l}_{i}", [128, 512], F32)[0] for i in range(2)]
        sig_b = gbufs("sig", 2)
        utail, _, _ = alloc(f"utail{l}", [128, 4, 3, 32], F32)
        utail_b = [[gbuf(f"utail{j}_{q}") for q in range(3)] for j in range(4)]
        ctail, _, _ = alloc(f"ctail{l}", [32, 512], F32)
        ctail_b = gbuf("ctail")
        chist, _, _ = alloc(f"chist{l}", [32, 2, 512], F32)
        chist_b = gbuf("chist")
        cwst, _, _ = alloc(f"cwst{l}", [32, 512], F32)
        cwst_b = gbuf("cwst")

        fw.dma("sp", lambda h, s: (h.dma_start(out=cosT[:], in_=cos_d[:, :, :]).then_inc(s, 16),
                                   h.dma_start(out=sinT[:], in_=sin_d[:, :, :]).then_inc(s, 16))[1],
               rope_b, "w", writes=[rope_b], n=2)
        fw.dma("sp", lambda h, s: (h.dma_start(out=chist[:HIST, 0, :], in_=cc_d[l, 0]).then_inc(s, 16),
                                   h.dma_start(out=chist[:HIST, 1, :], in_=cc_d[l, 1]).then_inc(s, 16))[1],
               chist_b, "w", writes=[chist_b], n=2)
        fw.op("pool", lambda h: h.memset(cwst[:, :], 0.0), writes=[cwst_b])
        fw.dma("sp", lambda h, s: h.dma_start(out=cwst[:CW, :], in_=convw_d[l]).then_inc(s, 16),
               cwst_b, "w", writes=[cwst_b], n=1)
        fw.op("pool", lambda h: h.memset(u_p[:, :, 0:HIST], 0.0), writes=[u_b[j][6] for j in range(4)])
        if STOP_AFTER == "A0pre1":
            return
        for s_ in range(2):
            bi = next_bank()

            def trh(h, s_=s_, bi=bi):
                ins = None
                for j in range(4):
                    ins = h.transpose(psb[bi][:, j * 32:j * 32 + HIST], chist[:HIST, s_, j * 128:(j + 1) * 128], identf[:HIST, :HIST])
                return ins
            fw.op("pe", trh, reads=[chist_b, const_b], writes=[PB[bi]])
            fw.op("act", lambda h, s_=s_, bi=bi: h.copy(u_s[:, s_, :, 0:HIST], psb[bi][:, 0:128].rearrange("p (j c) -> p j c", j=4)[:, :, 0:HIST]),
                  reads=[PB[bi]], writes=[u_b[j][6] for j in range(4)])
        if STOP_AFTER == "A0pre2":
            return
        bi = next_bank()

        def trw(h, bi=bi):
            ins = None
            for j in range(4):
                ins = h.transpose(psb[bi][:, j * 32:j * 32 + 32], cwst[:32, j * 128:(j + 1) * 128], identf[:32, :32])
            return ins
        fw.op("pe", trw, reads=[cwst_b, const_b], writes=[PB[bi]])
        fw.op("act", lambda h, bi=bi: h.copy(convwT[:, :, :], psb[bi][:, 0:128].rearrange("p (j c) -> p j c", j=4)),
              reads=[PB[bi]], writes=[par_b])

        def tokmajor_mm(t, slot):
            rows = trows(t); c0 = tcol(t)
            bi = next_bank()
            wt = wring[slot]

            def mm(h):
                ins = None
                for kc in range(8):
                    ins = h.matmul(psb[bi][:rows, :], actT[:, kc, c0:c0 + rows], wt[:, kc, :], start=(kc == 0), stop=(kc == 7))
                return ins
            fw.op("pe", mm, reads=[actT_b[c][tt] for c in range(8) for tt in cover(t)] + [wring_b[slot]], writes=[PB[bi]])
            return bi

        def rope_mul(t, bi, par):
            rows = trows(t)
            ps4 = psb[bi][:rows, :].rearrange("p (h two i) -> p h two i", two=2, i=32)
            ps3 = psb[bi][:rows, :].rearrange("p (g i) -> p g i", i=32)
            cb16 = cosT[:rows, t, :].unsqueeze(1).to_broadcast([rows, 16, 32])
            sb8 = sinT[:rows, t, :].unsqueeze(1).to_broadcast([rows, 8, 32])
            a = t1[par]; b = t2[par]
            a3 = a[:rows, :].rearrange("p (g i) -> p g i", i=32)
            b4 = b[:rows, :].rearrange("p (h two i) -> p h two i", two=2, i=32)
            fw.op("dve", lambda h: h.tensor_tensor(a3, ps3, cb16, ALU.mult), reads=[PB[bi], rope_b], writes=[t1_b[par]])
            fw.op("dve", lambda h: h.tensor_tensor(b4[:, :, 0, :], ps4[:, :, 1, :], sb8, ALU.mult), reads=[PB[bi], rope_b], writes=[t2_b[par]])
            fw.op("dve", lambda h: h.tensor_tensor(b4[:, :, 1, :], ps4[:, :, 0, :], sb8, ALU.mult), reads=[PB[bi], rope_b], writes=[t2_b[par]])

        def rope_comb(t, out_ap, out_bufs, par):
            rows = trows(t)
            a = t1[par]; b = t2[par]
            a4 = a[:rows, :].rearrange("p (h two i) -> p h two i", two=2, i=32)
            b4 = b[:rows, :].rearrange("p (h two i) -> p h two i", two=2, i=32)
            o4 = out_ap.rearrange("p (h two i) -> p h two i", two=2, i=32)
            fw.op("pool", lambda h: h.tensor_tensor(o4[:, :, 0, :], a4[:, :, 0, :], b4[:, :, 0, :], ALU.subtract),
                  reads=[t1_b[par], t2_b[par]], writes=out_bufs)
            fw.op("pool", lambda h: h.tensor_tensor(o4[:, :, 1, :], a4[:, :, 1, :], b4[:, :, 1, :], ALU.add),
                  reads=[t1_b[par], t2_b[par]], writes=out_bufs)

        def to_featmajor_tr(t, src_bf, src_buf, st):
            rows = trows(t)
            bi = next_bank()
            st["bi"] = bi
            pst = psb[bi][:].bitcast(BF16)

            def tr(h):
                ins = None
                for c in range(4):
                    ins = h.transpose(pst[:, c * 128:c * 128 + rows], src_bf[:rows, c * 128:(c + 1) * 128], identb[:rows, :rows])
                return ins
            fw.op("pe", tr, reads=[src_buf, const_b], writes=[PB[bi]])

        def to_featmajor_ev(t, dst, dst_bufs, st):
            rows = trows(t); c0 = tcol(t)
            bi = st["bi"]
            pst = psb[bi][:].bitcast(BF16)
            src = pst[:, 0:512].rearrange("p (c r) -> p c r", c=4)[:, :, :rows]
            fw.op("act", lambda h: h.copy(dst[:, :, c0:c0 + rows], src), reads=[PB[bi]], writes=dst_bufs)

        def to_featmajor(t, src_bf, src_buf, dst, dst_bufs_fn):
            rows = trows(t); c0 = tcol(t)
            bi = next_bank()
            pst = psb[bi][:].bitcast(BF16)

            def tr(h):
                ins = None
                for c in range(4):
                    ins = h.transpose(pst[:, c * 128:c * 128 + rows], src_bf[:rows, c * 128:(c + 1) * 128], identb[:rows, :rows])
                return ins
            fw.op("pe", tr, reads=[src_buf, const_b], writes=[PB[bi]])
            src = pst[:, 0:512].rearrange("p (c r) -> p c r", c=4)[:, :, :rows]
            fw.op("act", lambda h: h.copy(dst[:, :, c0:c0 + rows], src), reads=[PB[bi]], writes=dst_bufs_fn(t))

        if STOP_AFTER == "A0pre":
            return
        slot = acquire_piece(); prefetch_pieces()
        items = []
        for t in range(NMT):
            st = {}

            def a0(t=t, slot=slot, st=st):
                st["mm"] = tokmajor_mm(t, slot)

            def a1(t=t, st=st):
                rope_mul(t, st["mm"], t % 2)

            def a2(t=t):
                rows = trows(t); r3 = t % 3
                rope_comb(t, qb[r3][:rows, :], [qb_b[r3]], t % 2)

            def a3(t=t, st=st):
                r3 = t % 3
                to_featmajor_tr(t, qb[r3], qb_b[r3], st)

            def a4(t=t, st=st):
                to_featmajor_ev(t, big2[:, 0:4, :], [big2_b[c][tt] for c in range(4) for tt in cover(t)], st)
            items.append([a0, a1, a2, a3, a4])
        kslot = {}
        for t in range(NMT):
            st = {}
            ix = NMT + t

            def a0(t=t, st=st):
                if "s" not in kslot:
                    kslot["s"] = acquire_piece(); prefetch_pieces()
                st["mm"] = tokmajor_mm(t, kslot["s"])

            def a1(t=t, st=st, ix=ix):
                rope_mul(t, st["mm"], ix % 2)

            def a2(t=t, ix=ix):
                rows = trows(t); r3 = ix % 3
                rope_comb(t, kf[r3][:rows, :], [kf_b[r3]], ix % 2)

            def a2b(t=t, ix=ix):
                rows = trows(t); r3 = ix % 3; c0 = tcol(t)
                fw.dma("sp", lambda h, s: h.dma_start(out=nk_d[l, c0:c0 + rows, :], in_=kf[r3][:rows, :]).then_inc(s, 16),
                       kf_b[r3], "r", reads=[kf_b[r3]])
                fw.op("act", lambda h: h.copy(qb[r3][:rows, :], kf[r3][:rows, :]), reads=[kf_b[r3]], writes=[qb_b[r3]])

            def a3(t=t, st=st, ix=ix):
                r3 = ix % 3
                to_featmajor_tr(t, qb[r3], qb_b[r3], st)

            def a4(t=t, st=st):
                to_featmajor_ev(t, big2[:, 4:8, :], [big2_b[4 + c][tt] for c in range(4) for tt in cover(t)], st)
            items.append([a0, a1, a2, a2b, a3, a4])
        vslot = {}
        for t in range(NT):
            def v0(t=t):
                if "s" not in vslot:
                    vslot["s"] = acquire_piece(); prefetch_pieces()
                _GEOM["merged"] = False
                rows = trows(t); par = t % 2; c0 = tcol(t)
                bi = tokmajor_mm(t, vslot["s"])
                _GEOM["merged"] = True
                fw.op("act", lambda h: h.copy(vf[par][:rows, :], psb[bi][:rows, :]), reads=[PB[bi]], writes=[vf_b[par]])
                fw.op("dve", lambda h: h.tensor_copy(vbf[:rows, t, :], psb[bi][:rows, :]), reads=[PB[bi]], writes=[vbf_b[t]])
                fw.dma("sp", lambda h, s: h.dma_start(out=nv_d[l, c0:c0 + rows, :], in_=vf[par][:rows, :]).then_inc(s, 16),
                       vf_b[par], "r", reads=[vf_b[par]])
            items.append([v0])
        run_multi(items)
        if STOP_AFTER == "A0v":
            return
        glu_deferred = []
        for half in range(2):
            slot = acquire_piece(); prefetch_pieces()
            wt = wring[slot]
            for jj in range(2):
                j = 2 * half + jj
                for bi_, (c0, n) in enumerate(BLOCKS):
                    tl = block_tiles(bi_)
                    ba = next_bank(); bg = next_bank()

                    def mma(h, ba=ba, jj=jj, c0=c0, n=n, wt=wt):
                        ins = None
                        for kc in range(8):
                            ins = h.matmul(psb[ba][:, :n], wt[:, kc, jj * 128:(jj + 1) * 128], actT[:, kc, c0:c0 + n], start=(kc == 0), stop=(kc == 7))
                        return ins

                    def mmg(h, bg=bg, jj=jj, c0=c0, n=n, wt=wt):
                        ins = None
                        for kc in range(8):
                            ins = h.matmul(psb[bg][:, :n], wt[:, kc, 256 + jj * 128:256 + (jj + 1) * 128], actT[:, kc, c0:c0 + n], start=(kc == 0), stop=(kc == 7))
                        return ins
                    rd = [actT_b[c][t] for c in range(8) for t in tl] + [wring_b[slot]]
                    fw.op("pe", mma, reads=rd, writes=[PB[ba]])
                    fw.op("pe", mmg, reads=rd, writes=[PB[bg]])
                    par = bi_ % 2
                    fw.op("act", lambda h, par=par, bg=bg, n=n: h.activation(sig[par][:, :n], psb[bg][:, :n], AF.Sigmoid),
                          reads=[PB[bg]], writes=[sig_b[par]])
                    for fn_ in glu_deferred:
                        fn_()
                    del glu_deferred[:]
                    if bi_ < 4:
                        fw.op("dve", lambda h, par=par, ba=ba, j=j, c0=c0: h.tensor_tensor(u_p[:, j, HIST + c0:HIST + c0 + 512], psb[ba][:, :512], sig[par][:, :512], ALU.mult),
                              reads=[PB[ba], sig_b[par]], writes=[u_b[j][bi_]])
                        if bi_ == 3:
                            fw.op("dve", lambda h, par=par, ba=ba, j=j: h.tensor_tensor(utail[:, j, 0, 0:HIST], psb[ba][:, 512 - HIST:512], sig[par][:, 512 - HIST:512], ALU.mult),
                                  reads=[PB[ba], sig_b[par]], writes=[utail_b[j][0]])
                    else:
                        for s_ in range(2):
                            fw.op("dve", lambda h, par=par, ba=ba, j=j, s_=s_: h.tensor_tensor(u_s[:, s_, j, HIST:HIST + TS], psb[ba][:, s_ * TS:(s_ + 1) * TS], sig[par][:, s_ * TS:(s_ + 1) * TS], ALU.mult),
                                  reads=[PB[ba], sig_b[par]], writes=[u_b[j][4 + s_]])
                            fw.op("dve", lambda h, par=par, ba=ba, j=j, s_=s_: h.tensor_tensor(utail[:, j, 1 + s_, 0:HIST], psb[ba][:, s_ * TS + TS - HIST:(s_ + 1) * TS], sig[par][:, s_ * TS + TS - HIST:(s_ + 1) * TS], ALU.mult),
                                  reads=[PB[ba], sig_b[par]], writes=[utail_b[j][1 + s_]])
                    if bi_ >= 3:
                        seqs = [0] if bi_ == 3 else [1, 2]
                        for sq in seqs:
                            def tail_out(sq=sq, j=j):
                                bt = next_bank()
                                fw.op("pe", lambda h: h.transpose(psb[bt][:HIST, 0:128], utail[:, j, sq, 0:HIST], identf[:, :]),
                                      reads=[utail_b[j][sq], const_b], writes=[PB[bt]])
                                fw.op("act", lambda h: h.copy(ctail[:HIST, j * 128:(j + 1) * 128], psb[bt][:HIST, 0:128]),
                                      reads=[PB[bt]], writes=[ctail_b])
                                fw.dma("sp", lambda h, s: h.dma_start(out=ncv_d[l, sq, :, j * 128:(j + 1) * 128], in_=ctail[:HIST, j * 128:(j + 1) * 128]).then_inc(s, 16),
                                       ctail_b, "r", reads=[ctail_b])
                            glu_deferred.append(tail_out)
        for fn_ in glu_deferred:
            fn_()
        del glu_deferred[:]

    _bufcache = {}

    def gbuf(name):
        if name not in _bufcache:
            _bufcache[name] = fw.buf(name)
        return _bufcache[name]

    def gbufs(name, n):
        return [gbuf(f"{name}{i}") for i in range(n)]

    def phase_B1(l):
        malloc_reset("M")
        csb = [alloc(f"csb{l}_{j}", [128, 512], F32)[0] for j in range(4)]
        csb_b = gbufs("csb", 4)
        csq = [alloc(f"csq{l}_{i}", [128, 512], F32)[0] for i in range(2)]
        csq_b = gbufs("csq", 2)
        mean, _, _ = alloc(f"cmean{l}", [128, 512], F32); mean_b = gbuf("cmean")
        rstd, _, _ = alloc(f"crstd{l}", [128, 512], F32); rstd_b = gbuf("crstd")
        zz = [alloc(f"cz{l}_{i}", [128, 512], F32)[0] for i in range(2)]
        zz_b = gbufs("cz", 2)
        dg_b = [[gbuf(f"dg{j}_{tap}") for tap in range(CW)] for j in range(4)]
        for j in range(4):
            for tap in range(CW):
                if tap % 2 == 0:
                    fw.op("dve", lambda h, j=j, tap=tap: h.tensor_scalar(diag[:, j, tap, :], identf[:, :], convwT[:, j, tap:tap + 1], None, ALU.mult),
                          reads=[const_b, par_b], writes=[dg_b[j][tap]])
                else:
                    fw.op("act", lambda h, j=j, tap=tap: h.activation(diag[:, j, tap, :], identf[:, :], AF.Copy, scale=convwT[:, j, tap:tap + 1]),
                          reads=[const_b, par_b], writes=[dg_b[j][tap]])
        seqs = []
        for bi_ in range(4):
            seqs.append(("p", bi_, 512, bi_ * 512))
        seqs.append(("s", 0, 2 * TS, TP))
        for (kind, idx, n, oc0) in seqs:
            banks = []
            for j in range(4):
                bk = next_bank(); banks.append(bk)
                if kind == "p":
                    src = lambda tap, j=j, idx=idx: u_p[:, j, idx * 512 + tap: idx * 512 + tap + 512]
                    rd = [u_b[j][idx], u_b[j][idx - 1] if idx > 0 else u_b[j][6]]
                else:
                    src = lambda tap, j=j: u_s[:, :, j, tap: tap + TS]
                    rd = [u_b[j][4], u_b[j][5], u_b[j][6]]

                def cv(h, bk=bk, j=j, src=src, n=n, kind=kind):
                    ins = None
                    for tap in range(CW):
                        o_ap = psb[bk][:, :n] if kind == "p" else psb[bk][:, :n].rearrange("p (a b) -> p a b", a=2)
                        ins = h.matmul(o_ap, diag[:, j, tap, :], src(tap), start=(tap == 0), stop=(tap == CW - 1))
                    return ins
                fw.op("pe", cv, reads=rd + dg_b[j], writes=[PB[bk]])
            bm = next_bank(); bq = next_bank()
            for j in range(4):
                bk = banks[j]
                fw.op("act", lambda h, j=j, bk=bk, n=n: h.activation(csb[j][:, :n], psb[bk][:, :n], AF.Identity, bias=cvec[:, 0, j:j + 1], scale=1.0),
                      reads=[PB[bk], par_b], writes=[csb_b[j]])
                fw.op("act", lambda h, j=j, bk=bk, n=n: h.activation(csq[j % 2][:, :n], psb[bk][:, :n], AF.Square, bias=cvec[:, 0, j:j + 1], scale=1.0),
                      reads=[PB[bk], par_b], writes=[csq_b[j % 2]])
                fw.op("pe", lambda h, j=j, n=n, bm=bm: h.matmul(psb[bm][:, :n], ones512[:, :], csb[j][:, :n], start=(j == 0), stop=(j == 3)),
                      reads=[csb_b[j], const_b], writes=[PB[bm]])
                fw.op("pe", lambda h, j=j, n=n, bq=bq: h.matmul(psb[bq][:, :n], ones512[:, :], csq[j % 2][:, :n], start=(j == 0), stop=(j == 3)),
                      reads=[csq_b[j % 2], const_b], writes=[PB[bq]])
            fw.op("act", lambda h, n=n, bm=bm: h.copy(mean[:, :n], psb[bm][:, :n]), reads=[PB[bm]], writes=[mean_b])
            fw.op("dve", lambda h, n=n: h.tensor_tensor(rstd[:, :n], mean[:, :n], mean[:, :n], ALU.mult), reads=[mean_b], writes=[rstd_b])
            fw.op("dve", lambda h, n=n, bq=bq: h.tensor_tensor(rstd[:, :n], psb[bq][:, :n], rstd[:, :n], ALU.subtract), reads=[PB[bq], rstd_b], writes=[rstd_b])
            fw.op("act", lambda h, n=n: h.activation(rstd[:, :n], rstd[:, :n], AF.Ln, bias=epsln[:, 0:1], scale=1.0), reads=[rstd_b, const_b], writes=[rstd_b])
            fw.op("act", lambda h, n=n: h.activation(rstd[:, :n], rstd[:, :n], AF.Exp, scale=-0.5), reads=[rstd_b], writes=[rstd_b])
            otl = cols_tiles(oc0, n)
            for j in range(4):
                zi = j % 2
                fw.op("dve", lambda h, j=j, zi=zi, n=n: h.tensor_tensor(zz[zi][:, :n], csb[j][:, :n], mean[:, :n], ALU.subtract),
                      reads=[csb_b[j], mean_b], writes=[zz_b[zi]])
                fw.op("dve", lambda h, zi=zi, n=n: h.tensor_tensor(zz[zi][:, :n], zz[zi][:, :n], rstd[:, :n], ALU.mult),
                      reads=[zz_b[zi], rstd_b], writes=[zz_b[zi]])
                fw.op("act", lambda h, j=j, zi=zi, n=n, oc0=oc0: h.activation(actT[:, 4 + j, oc0:oc0 + n], zz[zi][:, :n], AF.Silu, bias=cvec[:, 2, j:j + 1], scale=cvec[:, 1, j:j + 1]),
                      reads=[zz_b[zi], par_b], writes=[actT_b[4 + j][t] for t in otl])

    ST_BANKS = [0, 1, 2, 7]
    O_BANKS = [3, 4]
    S_BANKS = [5, 6]

    def phase_B2(l):
        malloc_reset("XU")
        NKV = 4
        Kst = [alloc(f"Kst{l}_{i}", [128, 16, 128], BF16)[0] for i in range(NKV)]; Kst_b = gbufs("Kst", NKV)
        Vst = [alloc(f"Vst{l}_{i}", [128, 16, 128], BF16)[0] for i in range(NKV)]; Vst_b = gbufs("Vst", NKV)
        KT = [alloc(f"KT{l}_{i}", [128, 2048], BF16)[0] for i in range(2)]; KT_b = gbufs("KT", 2)
        PT = [alloc(f"PT{l}_{i}", [128, 512], BF16)[0] for i in range(4)]; PT_b = gbufs("PT", 4)
        malloc_reset("M")
        r1, _, _ = alloc(f"r1_{l}", [128, 512], F32); r2, _, _ = alloc(f"r2_{l}", [128, 512], F32)
        ta, _, _ = alloc(f"ta_{l}", [128, 512], F32); tb, _, _ = alloc(f"tb_{l}", [128, 512], F32)
        sq, _, _ = alloc(f"sq_{l}", [128, 512], F32); rr, _, _ = alloc(f"rr_{l}", [128, 512], F32)
        r1_b = gbuf("r1"); r2_b = gbuf("r2"); ta_b = gbuf("ta"); tb_b = gbuf("tb"); sq_b = gbuf("sq"); rr_b = gbuf("rr")
        stc = [0]; ptc = [0]

        pipe = []
        deferred = []

        def attend(j, qc0, nq, keytiles, q_reads, out_bufs):
            nkt = len(keytiles)
            for sub in range(2):
                bO = O_BANKS[sub]; bS = S_BANKS[sub]
                ps_ = slice(sub * 64, (sub + 1) * 64)
                per_bank = (512 // nq) if nq <= 64 else 1
                groups = []
                cur_g = []
                for kt_ in keytiles:
                    if cur_g and (len(cur_g) >= per_bank or kt_["nk"] != cur_g[0]["nk"]):
                        groups.append(cur_g); cur_g = []
                    cur_g.append(kt_)
                if cur_g:
                    groups.append(cur_g)
                done = 0
                for gi, g in enumerate(groups):
                    is_last = (sub == 1 and gi == len(groups) - 1)

                    def front(g=g, sub=sub, ps_=ps_, st={}):
                        bST = ST_BANKS[stc[0] % 4]; stc[0] += 1
                        pi = ptc[0] % 4; ptc[0] += 1
                        st["pi"] = pi
                        nk = g[0]["nk"]

                        def qk(h):
                            ins = None
                            for i, kt_ in enumerate(g):
                                c0 = kt_["c0"]
                                ins = h.matmul(psb[bST][:nk, i * nq + c0:(i + 1) * nq], kt_["kT"](sub), big2[ps_, j, qc0 + c0:qc0 + nq], start=True, stop=True)
                            return ins
                        rds = list(q_reads)
                        for kt_ in g:
                            rds += kt_["kreads"]
                        fw.op("pe", qk, reads=rds, writes=[PB[bST]])
                        lo = g[0]["c0"]; hi = len(g) * nq
                        fw.op("act", lambda h: h.activation(PT[pi][:nk, lo:hi], psb[bST][:nk, lo:hi], AF.Exp, scale=0.125),
                              reads=[PB[bST]], writes=[PT_b[pi]])
                        if g[0]["diag"]:
                            fw.op("dve", lambda h: h.memset(PT[pi][64:128, lo:lo + 64], 0.0), writes=[PT_b[pi]])

                    def back(g=g, done=done, bO=bO, bS=bS, is_last=is_last, st=None):
                        pass
                    st_ = {}
                    front_fn = (lambda f=front, st_=st_: f(st=st_))

                    def back_fn(g=g, done=done, bO=bO, bS=bS, is_last=is_last, st_=st_):
                        pi = st_["pi"]
                        nk = g[0]["nk"]

                        def av(h):
                            ins = None
                            for i, kt_ in enumerate(g):
                                c0 = kt_["c0"]
                                first = (done + i == 0); last = (done + i == nkt - 1)
                                h.matmul(psb[bO][:, c0:nq], kt_["v"], PT[pi][:nk, i * nq + c0:(i + 1) * nq], start=first, stop=last)
                                ins = h.matmul(psb[bS][:, c0:nq], onesb[:nk, :], PT[pi][:nk, i * nq + c0:(i + 1) * nq], start=first, stop=last)
                            return ins
                        rds = [PT_b[pi], const_b]
                        for kt_ in g:
                            rds += kt_["vreads"]
                        fw.op("pe", av, reads=rds, writes=[PB[bO], PB[bS]])
                        if is_last:
                            normalize(j, qc0, nq, out_bufs)
                    pipe.append((front_fn, back_fn))
                    done += len(g)

        def normalize(j, qc0, n, out_bufs):
            bO0, bO1 = O_BANKS; bS0, bS1 = S_BANKS
            fw.op("act", lambda h: h.activation(r1[:, :n], psb[bS0][:, :n], AF.Ln), reads=[PB[bS0]], writes=[r1_b])
            fw.op("act", lambda h: h.activation(r2[:, :n], psb[bS1][:, :n], AF.Ln), reads=[PB[bS1]], writes=[r2_b])
            fw.op("dve", lambda h: h.tensor_copy(ta[:, :n], psb[bO0][:, :n]), reads=[PB[bO0]], writes=[ta_b])
            fw.op("dve", lambda h: h.tensor_copy(tb[:, :n], psb[bO1][:, :n]), reads=[PB[bO1]], writes=[tb_b])
            fw.op("act", lambda h: h.activation(r1[:, :n], r1[:, :n], AF.Exp, scale=-1.0), reads=[r1_b], writes=[r1_b])
            fw.op("act", lambda h: h.activation(r2[:, :n], r2[:, :n], AF.Exp, scale=-1.0), reads=[r2_b], writes=[r2_b])
            fw.op("dve", lambda h: h.tensor_tensor(ta[:, :n], ta[:, :n], r1[:, :n], ALU.mult), reads=[ta_b, r1_b], writes=[ta_b])
            fw.op("dve", lambda h: h.tensor_tensor(tb[:, :n], tb[:, :n], r2[:, :n], ALU.mult), reads=[tb_b, r2_b], writes=[tb_b])
            fw.op("dve", lambda h: h.scalar_tensor_tensor(ta[:, :n], tb[:, :n], lamw[:, 5:6], ta[:, :n], ALU.mult, ALU.add),
                  reads=[tb_b, ta_b, par_b], writes=[ta_b])
            fw.op("act", lambda h: h.activation(sq[:, :n], ta[:, :n], AF.Square), reads=[ta_b], writes=[sq_b])
            deferred.append([4, lambda: normalize2(j, qc0, n, out_bufs)])

        def normalize2(j, qc0, n, out_bufs):
            R_BANK = ST_BANKS[stc[0] % 4]; stc[0] += 1
            fw.op("pe", lambda h: h.matmul(psb[R_BANK][:, :n], ones128[:, :], sq[:, :n], start=True, stop=True), reads=[sq_b, const_b], writes=[PB[R_BANK]])
            fw.op("act", lambda h: h.activation(rr[:, :n], psb[R_BANK][:, :n], AF.Ln, bias=epsln[:, 0:1], scale=1.0), reads=[PB[R_BANK], const_b], writes=[rr_b])
            fw.op("act", lambda h: h.activation(rr[:, :n], rr[:, :n], AF.Exp, scale=-0.5), reads=[rr_b], writes=[rr_b])
            fw.op("dve", lambda h: h.tensor_tensor(ta[:, :n], ta[:, :n], rr[:, :n], ALU.mult), reads=[ta_b, rr_b], writes=[ta_b])
            fw.op("dve", lambda h: h.tensor_scalar(actT[:, j, qc0:qc0 + n], ta[:, :n], subg[:, 1:2], None, ALU.mult),
                  reads=[ta_b, par_b], writes=out_bufs)

        def run_pipe(depth=3):
            n = len(pipe)
            for i in range(n + depth):
                if i < n:
                    pipe[i][0]()
                if i >= depth:
                    pipe[i - depth][1]()
                for d in list(deferred):
                    d[0] -= 1
                    if d[0] <= 0:
                        deferred.remove(d)
                        d[1]()
            for d in list(deferred):
                deferred.remove(d)
                d[1]()
            del pipe[:]

        combos = [(s_, j) for s_ in range(2) for j in range(4)]

        def sample_load(ci):
            s_, j = combos[ci]; par = ci % NKV
            fw.dma("pool", lambda h, s: h.dma_start(out=Kst[par][:], in_=ck_d[l, s_, :, j * 128:(j + 1) * 128].rearrange("(kt p) c -> p kt c", p=128)).then_inc(s, 16),
                   Kst_b[par], "w", writes=[Kst_b[par]])
            fw.dma("pool", lambda h, s: h.dma_start(out=Vst[par][:], in_=cv_d[l, s_, :, j * 128:(j + 1) * 128].rearrange("(kt p) c -> p kt c", p=128)).then_inc(s, 16),
                   Vst_b[par], "w", writes=[Vst_b[par]])

        def sample_prep(ci):
            s_, j = combos[ci]; par = ci % 2; kv = ci % NKV
            for hh in range(2):
                bk = next_bank()
                pst = psb[bk][:].bitcast(BF16)

                def trk(h, hh=hh, pst=pst):
                    ins = None
                    for i in range(8):
                        ins = h.transpose(pst[:, i * 128:(i + 1) * 128], Kst[kv][:, hh * 8 + i, :], identb[:, :])
                    return ins
                fw.op("pe", trk, reads=[Kst_b[kv], const_b], writes=[PB[bk]])
                fw.op("dve", lambda h, hh=hh, pst=pst: h.tensor_copy(KT[par][:, hh * 1024:(hh + 1) * 1024], pst[:, :]),
                      reads=[PB[bk]], writes=[KT_b[par]])

        def sample_attend(ci):
            s_, j = combos[ci]; par = ci % 2; kv = ci % NKV
            keytiles = []
            for kt in range(16):
                keytiles.append(dict(kT=(lambda sub, kt=kt: KT[par][sub * 64:(sub + 1) * 64, kt * 128:(kt + 1) * 128]),
                                     v=Vst[kv][:, kt, :], nk=128, c0=0, diag=False, kreads=[KT_b[par]], vreads=[Vst_b[kv]]))
            nc0 = TP + s_ * TS
            keytiles.append(dict(kT=(lambda sub: big2[sub * 64:(sub + 1) * 64, 4 + j, nc0:nc0 + TS]),
                                 v=vbf[:TS, 16 + s_, j * 128:(j + 1) * 128], nk=TS, c0=0, diag=False,
                                 kreads=[big2_b[4 + j][16 + s_]], vreads=[vbf_b[16 + s_]]))
            attend(j, nc0, TS, keytiles, [big2_b[j][16 + s_]], [actT_b[j][16 + s_]])
            run_pipe()

        for ci in range(NKV):
            sample_load(ci)
        for j in range(4):
            for qb in range(4):
                keytiles = []
                for kt in range(4 * qb + 4):
                    c0 = max(0, kt * 128 - qb * 512)
                    keytiles.append(dict(kT=(lambda sub, j=j, kt=kt: big2[sub * 64:(sub + 1) * 64, 4 + j, kt * 128:(kt + 1) * 128]),
                                         v=vbf[:, kt, j * 128:(j + 1) * 128], nk=128, c0=c0, diag=(kt >= 4 * qb),
                                         kreads=[big2_b[4 + j][kt]], vreads=[vbf_b[kt]]))
                attend(j, qb * 512, 512, keytiles, [big2_b[j][4 * qb + i] for i in range(4)], [actT_b[j][4 * qb + i] for i in range(4)])
        run_pipe()

        sample_prep(0)
        for ci in range(len(combos)):
            if ci + 1 < len(combos):
                sample_prep(ci + 1)
            sample_attend(ci)
            if ci + NKV < len(combos):
                sample_load(ci + NKV)

    ln_state = {}

    def ln_setup():
        malloc_reset("M")
        gB, _, _ = alloc("lngB", [128, D], F32); bB, _, _ = alloc("lnbB", [128, D], F32)
        st = [alloc(f"lnst{i}", [128, 2, 6], F32)[0] for i in range(6)]
        mv = [alloc(f"lnmv{i}", [128, 4], F32)[0] for i in range(6)]
        rl = [alloc(f"rl{i}", [128, 512], F32)[0] for i in range(2)]
        ln_state.update(gB=gB, bB=bB, st=st, mv=mv, rl=rl, lnp_b=gbuf("lnp"), st_b=gbufs("lnst", 6), mv_b=gbufs("lnmv", 6), rl_b=gbufs("rl", 2))

    def ln_load(g_d, b_d):
        L = ln_state
        fw.dma("sp", lambda h, s: (h.dma_start(out=L["gB"][:], in_=g_d.partition_broadcast(128)).then_inc(s, 16),
                                   h.dma_start(out=L["bB"][:], in_=b_d.partition_broadcast(128)).then_inc(s, 16))[1],
               L["lnp_b"], "w", writes=[L["lnp_b"]], n=2)

    def ln_stats(t, par):
        L = ln_state
        rows = trows(t)
        st = L["st"][par]; mv = L["mv"][par]; st_b = L["st_b"][par]; mv_b = L["mv_b"][par]
        xb_ = xres_b[t]
        fw.op("dve", lambda h: h.bn_stats(st[:rows, 0, :], xres[:rows, t, 0:512]), reads=[xb_], writes=[st_b])
        fw.op("dve", lambda h: h.bn_stats(st[:rows, 1, :], xres[:rows, t, 512:1024]), reads=[xb_], writes=[st_b])
        fw.op("dve", lambda h: h.bn_aggr(mv[:rows, 0:2], st[:rows].rearrange("p a b -> p (a b)")), reads=[st_b], writes=[mv_b])

    def ln_rstd(t, par):
        L = ln_state
        rows = trows(t)
        mv = L["mv"][par]; mv_b = L["mv_b"][par]
        fw.op("act", lambda h: h.activation(mv[:rows, 2:3], mv[:rows, 1:2], AF.Ln, bias=epsln[:rows, 0:1], scale=1.0), reads=[mv_b, const_b], writes=[mv_b])
        fw.op("act", lambda h: h.activation(mv[:rows, 3:4], mv[:rows, 2:3], AF.Exp, scale=-0.5), reads=[mv_b], writes=[mv_b])

    def ln_nmr(t, par):
        L = ln_state
        rows = trows(t)
        mv = L["mv"][par]; mv_b = L["mv_b"][par]
        fw.op("dve", lambda h: h.scalar_tensor_tensor(mv[:rows, 2:3], mv[:rows, 0:1], -1.0, mv[:rows, 3:4], ALU.mult, ALU.mult), reads=[mv_b], writes=[mv_b])

    def ln_apply(t, par):
        L = ln_state
        rows = trows(t)
        mv = L["mv"][par]; mv_b = L["mv_b"][par]
        xb_ = xres_b[t]
        fw.op("act", lambda h: h.activation(xres[:rows, t, :], xres[:rows, t, :], AF.Identity, bias=mv[:rows, 2:3], scale=mv[:rows, 3:4]),
              reads=[xb_, mv_b], writes=[xb_])

    def ln_norm(t, par):
        ln_rstd(t, par); ln_nmr(t, par); ln_apply(t, par)

    def ln_gain(t):
        L = ln_state
        rows = trows(t)
        xb_ = xres_b[t]
        e1 = "pool" if (t % 2 == 1) else "dve"
        fw.op(e1, lambda h: h.tensor_tensor(xres[:rows, t, :], xres[:rows, t, :], L["gB"][:rows, :], ALU.mult), reads=[xb_, L["lnp_b"]], writes=[xb_])

    def ln_bias(t, use_pool=False):
        L = ln_state
        rows = trows(t)
        xb_ = xres_b[t]
        e2 = "pool" if use_pool else "dve"
        fw.op(e2, lambda h: h.tensor_tensor(xres[:rows, t, :], xres[:rows, t, :], L["bB"][:rows, :], ALU.add), reads=[xb_, L["lnp_b"]], writes=[xb_])

    def ln_affine(t, use_pool=False):
        ln_gain(t); ln_bias(t, use_pool)

    def layer_norm(t, par, use_pool=False):
        ln_stats(t, par); ln_norm(t, par); ln_affine(t, use_pool)

    ffn_pre = {}

    def ff1_block(hb, slot, fc, bi_, par):
        L = ln_state
        c0, n = BLOCKS[bi_]
        tl = block_tiles(bi_)
        bk = next_bank()
        w1t = wring[slot]

        def f1(h):
            ins = None
            for kc in range(8):
                ins = h.matmul(psb[bk][:, :n], w1t[:, kc, fc * 128:(fc + 1) * 128], actT[:, kc, c0:c0 + n], start=(kc == 0), stop=(kc == 7))
            return ins
        fw.op("pe", f1, reads=[actT_b[c][t] for c in range(8) for t in tl] + [wring_b[slot]], writes=[PB[bk]])
        fw.op("act", lambda h: h.activation(L["rl"][par][:, :n], psb[bk][:, :n], AF.Relu), reads=[PB[bk]], writes=[L["rl_b"][par]])
        fw.op("act", lambda h: h.activation(big2[:, hb * 4 + fc, c0:c0 + n], L["rl"][par][:, :n], AF.Square),
              reads=[L["rl_b"][par]], writes=[big2_b[hb * 4 + fc][t] for t in tl])

    def phase_C1(l):
        ln_setup()
        ln_load(ln1g_d[l], ln1b_d[l])
        s0 = acquire_piece(); s1 = acquire_piece(); sW1 = acquire_piece(); prefetch_pieces(3)
        slots = [s0, s1]
        ffn_pre["slot"] = sW1
        blk_last = {3: 0, 7: 1, 11: 2, 15: 3, 16: 4}
        xsrc = x_d if l == 0 else xsc_d
        items = []
        for t in range(NMT):
            def s0(t=t):
                rows = trows(t); c0 = tcol(t); par = t % 2
                rdx = [] if l == 0 else [xsc_b[t]]
                fw.dma("sp", lambda h, s: h.dma_start(out=xin[par][:rows, :], in_=xsrc[c0:c0 + rows, :]).then_inc(s, 16),
                       xin_b[par], "w", reads=rdx, writes=[xin_b[par]])
                for half in range(2):
                    bk = next_bank()
                    wt = wring[slots[half]]

                    def mm(h, bk=bk, wt=wt):
                        ins = None
                        for kc in range(8):
                            ins = h.matmul(psb[bk][:rows, :], actT[:, kc, c0:c0 + rows], wt[:, kc, :], start=(kc == 0), stop=(kc == 7))
                        return ins
                    fw.op("pe", mm, reads=[actT_b[c][tt] for c in range(8) for tt in cover(t)] + [wring_b[slots[half]]], writes=[PB[bk]])
                    fw.op("dve", lambda h, bk=bk, half=half: h.scalar_tensor_tensor(
                        xres[:rows, t, half * 512:(half + 1) * 512], xin[par][:rows, half * 512:(half + 1) * 512], ALPHA, psb[bk][:rows, :], ALU.mult, ALU.add),
                        reads=[PB[bk], xin_b[par]], writes=[xres_b[t]])
                ln_stats(t, t % 6)

            def s1(t=t):
                ln_rstd(t, t % 6)

            def s1b(t=t):
                ln_nmr(t, t % 6)

            def s1c(t=t):
                ln_apply(t, t % 6)

            def s2(t=t):
                ln_gain(t)

            def s2b(t=t):
                ln_bias(t, use_pool=True)

            def s2c(t=t):
                rows = trows(t)
                make_xT_front(t, xres[:rows, t, :], [xres_b[t]], t % 2)

            def s3(t=t):
                make_xT_back(t, t % 2, evac="act")
            def s4(t=t):
                if t in blk_last:
                    for fc in range(4):
                        ff1_block(0, sW1, fc, blk_last[t], fc % 2)
            items.append([s0, s1, s1b, s1c, s2, s2b, s2c, s3, s4])
        run_multi(items)

    def phase_C2(l):
        L = ln_state
        ln_load(ln2g_d[l], ln2b_d[l])
        last = (l == DEPTH - 1)
        pend = []
        for g in range(8):
            if g == 0:
                s1 = ffn_pre["slot"]; s2 = acquire_piece()
            else:
                s1 = acquire_piece(); s2 = acquire_piece()
            prefetch_pieces(4 if g == 7 else 2)
            hb = g % 2
            w2v = wring[s2][:].rearrange("p a b -> p (a b)").rearrange("p (f n) -> p f n", f=4)
            if g > 0:
                cntr = 0
                for fc in range(4):
                    for bi_ in range(len(BLOCKS)):
                        ff1_block(hb, s1, fc, bi_, cntr % 2); cntr += 1
            pend.append((hb, w2v, s2))
            if g == 6:
                continue
            if g == 7:
                prefetch_pieces(3)
            srcs = list(pend)
            del pend[:]
            first_acc = (g == 0)
            fin = (g == 7)
            stages = []
            for t in range(NMT):
                def front(t=t, srcs=srcs, first_acc=first_acc, fin=fin):
                    rows = trows(t); c0 = tcol(t)
                    nmm = 4 * len(srcs)
                    for half in range(2):
                        bk = next_bank()

                        def f2(h, bk=bk, half=half):
                            ins = None
                            i = 0
                            for (hb_, w2v_, _s) in srcs:
                                for fc in range(4):
                                    ins = h.matmul(psb[bk][:rows, :], big2[:, hb_ * 4 + fc, c0:c0 + rows], w2v_[:, fc, half * 512:(half + 1) * 512], start=(i == 0), stop=(i == nmm - 1))
                                    i += 1
                            return ins
                        rds = []
                        for (hb_, w2v_, s2_) in srcs:
                            rds += [big2_b[hb_ * 4 + fc][tt] for fc in range(4) for tt in cover(t)] + [wring_b[s2_]]
                        fw.op("pe", f2, reads=rds, writes=[PB[bk]])
                        if first_acc:
                            fw.op("dve", lambda h, bk=bk, half=half: h.scalar_tensor_tensor(
                                xres[:rows, t, half * 512:(half + 1) * 512], xres[:rows, t, half * 512:(half + 1) * 512], ALPHA, psb[bk][:rows, :], ALU.mult, ALU.add),
                                reads=[PB[bk], xres_b[t]], writes=[xres_b[t]])
                        else:
                            fw.op("dve", lambda h, bk=bk, half=half: h.tensor_tensor(
                                xres[:rows, t, half * 512:(half + 1) * 512], xres[:rows, t, half * 512:(half + 1) * 512], psb[bk][:rows, :], ALU.add),
                                reads=[PB[bk], xres_b[t]], writes=[xres_b[t]])
                    if fin:
                        ln_stats(t, t % 6)

                def tl1(t=t):
                    ln_rstd(t, t % 6)

                def tl1b(t=t):
                    ln_nmr(t, t % 6)

                def tl1c(t=t):
                    ln_apply(t, t % 6)

                def tl1d(t=t):
                    ln_gain(t)

                def tl2(t=t):
                    rows = trows(t); c0 = tcol(t)
                    ln_bias(t, use_pool=True)
                    if last:
                        fw.dma("sp", lambda h, s: h.dma_start(out=y_d[c0:c0 + rows, :], in_=xres[:rows, t, :]).then_inc(s, 16),
                               xres_b[t], "r", reads=[xres_b[t]])
                    else:
                        fw.dma("sp", lambda h, s: h.dma_start(out=xsc_d[c0:c0 + rows, :], in_=xres[:rows, t, :]).then_inc(s, 16),
                               xres_b[t], "r", reads=[xres_b[t]], writes=[xsc_b[t]])
                        make_xT_front(t, xres[:rows, t, :], [xres_b[t]], t % 2)

                def tl3(t=t):
                    if not last:
                        make_xT_back(t, t % 2, evac="act")
                stages.append([front, tl1, tl1b, tl1c, tl1d, tl2, tl3] if fin else [front])
            run_multi(stages)
        prefetch_pieces(0)

    k_const()
    items0 = []
    for t in range(NMT):
        def p0(t=t):
            rows = trows(t); c0 = tcol(t); par = t % 2
            fw.dma("sp", lambda h, s: h.dma_start(out=xin[par][:rows, :], in_=x_d[c0:c0 + rows, :]).then_inc(s, 16),
                   xin_b[par], "w", writes=[xin_b[par]])
            make_xT_front(t, xin[par][:rows, :], [xin_b[par]], par)

        def p1(t=t):
            make_xT_back(t, t % 2, evac="dve")
        items0.append([p0, p1])
    run_multi(items0)
    for l in range(DEPTH):
        if STOP_AFTER == "X":
            break
        load_params(l)
        if STOP_AFTER == "P":
            break
        phase_A(l)
        fw.barrier()
        if STOP_AFTER is not None and STOP_AFTER.startswith("A%d" % l):
            break
        phase_B1(l)
        fw.barrier()
        if STOP_AFTER == "B1_%d" % l:
            break
        phase_B2(l)
        fw.barrier()
        if STOP_AFTER == "B2_%d" % l:
            break
        phase_C1(l)
        if STOP_AFTER == "C1_%d" % l:
            break
        phase_C2(l)
        fw.barrier()
        if STOP_AFTER == "C2_%d" % l:
            break

    fw.barrier()
    if DEBUG_DUMP:
        dbg_d = nc.dram_tensor("dbg", [128, 8 * TOK], BF16, kind="ExternalOutput").ap()
        dbgb = fw.buf("dbg")
        fw.dma("sp", lambda h, s: h.dma_start(out=dbg_d[:, :], in_=actT[:].rearrange("p a b -> p (a b)")).then_inc(s, 16), dbgb, "r", reads=[dbgb])
        fw.barrier()
    fw.emit()
    fw.close()
    return nc


_PROG = None


def _rope_tables():
    half = 32
    inv = 10000.0 ** (-np.arange(half, dtype=np.float64) / half)
    pos = np.zeros((NT, 128), np.float64)
    for t in range(NT):
        if t < 16:
            pos[t] = t * 128 + np.arange(128)
        else:
            pos[t, :64] = 2048 + np.arange(64)
            pos[t, 64:] = 2048 + np.arange(64)
    ang = pos[:, :, None] * inv[None, None, :]
    cos = np.cos(ang).astype(np.float32).transpose(1, 0, 2).copy()
    sin = np.sin(ang).astype(np.float32).transpose(1, 0, 2).copy()
    return cos, sin


def kernel(x_prompt, x_sample, cache_k, cache_v, cache_conv, w_in, lambda_q1, lambda_k1,
           lambda_q2, lambda_k2, subln_g, conv_w, conv_b, conv_ln_g, conv_ln_b, w_out,
           ln1_g, ln1_b, w_ff1, w_ff2, ln2_g, ln2_b):
    global _PROG
    f = lambda a: np.ascontiguousarray(np.asarray(a, dtype=np.float32))
    x_prompt = f(x_prompt); x_sample = f(x_sample)
    cache_k = f(cache_k); cache_v = f(cache_v); cache_conv = f(cache_conv)
    cos, sin = _rope_tables()
    shared = {
        "w_in": f(w_in), "w_out": f(w_out), "w_ff1": f(w_ff1), "w_ff2": f(w_ff2),
        "lq1": f(lambda_q1), "lk1": f(lambda_k1), "lq2": f(lambda_q2), "lk2": f(lambda_k2),
        "subln_g": f(subln_g), "conv_w": f(conv_w), "conv_b": f(conv_b),
        "conv_ln_g": f(conv_ln_g), "conv_ln_b": f(conv_ln_b),
        "ln1_g": f(ln1_g), "ln1_b": f(ln1_b), "ln2_g": f(ln2_g), "ln2_b": f(ln2_b),
        "rope_cos": cos, "rope_sin": sin,
    }
    in_maps = []
    for c in range(NCORE):
        m = dict(shared)
        m["x"] = np.ascontiguousarray(np.concatenate([x_prompt[c], x_sample[2 * c], x_sample[2 * c + 1]], axis=0))
        m["ck"] = np.ascontiguousarray(cache_k[:, 2 * c:2 * c + 2].reshape(DEPTH, 2, TP, 512))
        m["cv"] = np.ascontiguousarray(cache_v[:, 2 * c:2 * c + 2].reshape(DEPTH, 2, TP, 512))
        m["cc"] = np.ascontiguousarray(cache_conv[:, 2 * c:2 * c + 2])
        in_maps.append(m)
    if _PROG is None:
        _PROG = build_program()
    if DEBUG_CORES is not None:
        res = run_bass_kernel_spmd(_PROG, in_maps[:DEBUG_CORES], core_ids=list(range(DEBUG_CORES)))
        R = list(res.results) + [res.results[0]] * (NCORE - DEBUG_CORES)
        if DEBUG_DUMP:
            DEBUG_OUT["dbg"] = res.results[0]["dbg"]
    else:
        res = run_bass_kernel_spmd(_PROG, in_maps, core_ids=list(range(NCORE)))
        R = res.results
    y_prompt = np.stack([R[c]["y"][:TP] for c in range(NCORE)])
    y_sample = np.stack([R[c // 2]["y"][TP + (c % 2) * TS:TP + (c % 2 + 1) * TS] for c in range(2 * NCORE)])
    nk_p = np.stack([R[c]["nk"][:, :TP] for c in range(NCORE)], axis=1).reshape(DEPTH, NCORE, TP, 8, 64)
    nv_p = np.stack([R[c]["nv"][:, :TP] for c in range(NCORE)], axis=1).reshape(DEPTH, NCORE, TP, 4, 128)
    nc_p = np.stack([R[c]["ncv"][:, 0] for c in range(NCORE)], axis=1)
    nk_s = np.stack([R[c // 2]["nk"][:, TP + (c % 2) * TS:TP + (c % 2 + 1) * TS] for c in range(2 * NCORE)], axis=1).reshape(DEPTH, 2 * NCORE, TS, 8, 64)
    nv_s = np.stack([R[c // 2]["nv"][:, TP + (c % 2) * TS:TP + (c % 2 + 1) * TS] for c in range(2 * NCORE)], axis=1).reshape(DEPTH, 2 * NCORE, TS, 4, 128)
    nc_s = np.stack([R[c // 2]["ncv"][:, 1 + (c % 2)] for c in range(2 * NCORE)], axis=1)
    return (y_prompt, y_sample, np.ascontiguousarray(nk_p), np.ascontiguousarray(nv_p), np.ascontiguousarray(nc_p),
            np.ascontiguousarray(nk_s), np.ascontiguousarray(nv_s), np.ascontiguousarray(nc_s))
```

```python
import math
import numpy as np
import concourse.bass as bass
import concourse.mybir as mybir
from concourse.bass_utils import run_bass_kernel_spmd

F32 = mybir.dt.float32
BF16 = mybir.dt.bfloat16
AF = mybir.ActivationFunctionType
ALU = mybir.AluOpType
AX = mybir.AxisListType

D = 1024
DEPTH = 2
NCORE = 8
TP = 2048
TS = 64
TOK = TP + 2 * TS
NT = 18
DIN = 2560
DFF = 4096
CW = 31
HIST = 30
ALPHA = (2.0 * DEPTH) ** 0.25
LN_EPS = 1e-5
RMS_EPS = 1e-5
STOP_AFTER = None
DEBUG_CORES = None
DEBUG_DUMP = False
DEBUG_OUT = {}


def lambda_init(layer):
    return 0.8 - 0.6 * math.exp(-0.3 * layer)


_GEOM = {"merged": True}
NMT = 17


def trows(t):
    if _GEOM["merged"] and t == 16:
        return 128
    return 128 if t < 16 else 64


def tcol(t):
    return t * 128 if t < 16 else TP + (t - 16) * TS


def cover(t):
    return [16, 17] if (_GEOM["merged"] and t == 16) else [t]


BLOCKS = [(0, 512), (512, 512), (1024, 512), (1536, 512), (2048, 128)]


def block_tiles(bi):
    return [4 * bi + i for i in range(4)] if bi < 4 else [16, 17]


class Buf:
    __slots__ = ("name", "lw", "rd", "wsem", "wcnt", "rsem", "rcnt", "excl")

    def __init__(self, name):
        self.name = name
        self.excl = False
        self.lw = None
        self.rd = []
        self.wsem = None
        self.wcnt = 0
        self.rsem = None
        self.rcnt = 0


class Eng:
    def __init__(self, name, sem):
        self.name = name
        self.sem = sem
        self.n = 0
        self.ops = []
        self.waited = {}


class FW:
    def __init__(self, nc):
        self.nc = nc
        self._ctx = []
        self.engs = {}
        for nm in ("pe", "act", "dve", "pool", "sp"):
            self.engs[nm] = Eng(nm, self.new_sem("e_" + nm))
        self.all_bufs = []

    def new_sem(self, name):
        self._nsem = getattr(self, "_nsem", 0) + 1
        cm = self.nc.semaphore("%s_%d" % (name, self._nsem))
        s = cm.__enter__()
        self._ctx.append(cm)
        return s

    def buf(self, name):
        b = Buf(name)
        self.all_bufs.append(b)
        return b

    def bufs(self, name, n):
        return [self.buf(f"{name}{i}") for i in range(n)]

    def _deps(self, reads, writes, esem=None):
        deps = []
        for b in reads:
            if b.lw is not None:
                deps.append(b.lw)
            if b.excl:
                deps.extend(r for r in b.rd if r[0] is not esem)
        for b in writes:
            if b.lw is not None:
                deps.append(b.lw)
            deps.extend(b.rd)
        return deps

    def _filter(self, e, deps, skip_self=False):
        best = {}
        semobj = {}
        for (s, v) in deps:
            k = id(s)
            if v > best.get(k, 0):
                best[k] = v
                semobj[k] = s
        waits = []
        for k, v in best.items():
            if skip_self and semobj[k] is e.sem:
                continue
            if e.waited.get(k, 0) >= v:
                continue
            e.waited[k] = v
            waits.append((semobj[k], v))
        return waits

    def op(self, eng, fn, reads=(), writes=()):
        e = self.engs[eng]
        waits = self._filter(e, self._deps(reads, writes, e.sem), skip_self=(eng == "pe"))
        e.n += 1
        tok = (e.sem, e.n)
        e.ops.append((waits, fn, 1))
        for b in reads:
            b.rd.append(tok)
        for b in writes:
            b.lw = tok
            b.rd = []
        return tok

    def dma(self, q, fn, owner, kind, reads=(), writes=(), n=1):
        e = self.engs[q]
        waits = self._filter(e, self._deps(reads, writes))
        if kind == "w":
            if owner.wsem is None:
                owner.wsem = self.new_sem("dw_" + owner.name)
            sem = owner.wsem
            owner.wcnt += n
            tok = (sem, 16 * owner.wcnt)
        else:
            if owner.rsem is None:
                owner.rsem = self.new_sem("dr_" + owner.name)
            sem = owner.rsem
            owner.rcnt += n
            tok = (sem, 16 * owner.rcnt)
        e.ops.append((waits, (lambda h, fn=fn, sem=sem: fn(h, sem)), 0))
        for b in reads:
            b.rd.append(tok)
        for b in writes:
            b.lw = tok
            b.rd = []
        return tok

    def barrier(self):
        deps = []
        for e in self.engs.values():
            if e.n > 0:
                deps.append((e.sem, e.n))
        for b in self.all_bufs:
            if b.wsem is not None and b.wcnt:
                deps.append((b.wsem, 16 * b.wcnt))
            if b.rsem is not None and b.rcnt:
                deps.append((b.rsem, 16 * b.rcnt))
        for e in self.engs.values():
            waits = self._filter(e, deps)
            if waits:
                e.ops.append((waits, None, 0))

    def emit(self):
        nc = self.nc
        with nc.Block() as block:
            def run(e):
                def body(h):
                    for (waits, fn, inc) in e.ops:
                        for (s, v) in waits:
                            h.wait_ge(s, v)
                        if fn is None:
                            continue
                        ins = fn(h)
                        if inc:
                            ins.then_inc(e.sem, 1)
                return body
            block.tensor(run(self.engs["pe"]))
            block.scalar(run(self.engs["act"]))
            block.vector(run(self.engs["dve"]))
            block.gpsimd(run(self.engs["pool"]))
            block.sync(run(self.engs["sp"]))

    def close(self):
        for cm in reversed(self._ctx):
            cm.__exit__(None, None, None)
        self._ctx = []


def build_program():
    nc = bass.Bass("TRN2", target_bir_lowering=False)
    fw = FW(nc)

    def din(name, shape):
        return nc.dram_tensor(name, list(shape), F32, kind="ExternalInput").ap()

    def dout(name, shape):
        return nc.dram_tensor(name, list(shape), F32, kind="ExternalOutput").ap()

    x_d = din("x", [TOK, D])
    ck_d = din("ck", [DEPTH, 2, TP, 512])
    cv_d = din("cv", [DEPTH, 2, TP, 512])
    cc_d = din("cc", [DEPTH, 2, HIST, 512])
    win_d = din("w_in", [DEPTH, D, DIN])
    wout_d = din("w_out", [DEPTH, D, D])
    w1_d = din("w_ff1", [DEPTH, D, DFF])
    w2_d = din("w_ff2", [DEPTH, DFF, D])
    lq1_d = din("lq1", [DEPTH, 64]); lk1_d = din("lk1", [DEPTH, 64])
    lq2_d = din("lq2", [DEPTH, 64]); lk2_d = din("lk2", [DEPTH, 64])
    subg_d = din("subln_g", [DEPTH, 128])
    convw_d = din("conv_w", [DEPTH, CW, 512])
    convb_d = din("conv_b", [DEPTH, 512])
    clng_d = din("conv_ln_g", [DEPTH, 512]); clnb_d = din("conv_ln_b", [DEPTH, 512])
    ln1g_d = din("ln1_g", [DEPTH, D]); ln1b_d = din("ln1_b", [DEPTH, D])
    ln2g_d = din("ln2_g", [DEPTH, D]); ln2b_d = din("ln2_b", [DEPTH, D])
    cos_d = din("rope_cos", [128, NT, 32]); sin_d = din("rope_sin", [128, NT, 32])

    y_d = dout("y", [TOK, D])
    nk_d = dout("nk", [DEPTH, TOK, 512])
    nv_d = dout("nv", [DEPTH, TOK, 512])
    ncv_d = dout("ncv", [DEPTH, 3, HIST, 512])
    xsc_d = nc.dram_tensor("xsc", [TOK, D], F32, kind="Internal").ap()

    SB0 = 16512
    SBTOP = 229344
    cur = [SB0]
    lim = [SBTOP]

    def alloc(name, shape, dt, at=None):
        nbytes = int(np.prod(shape[1:])) * (4 if dt == F32 else 2)
        nbytes = (nbytes + 31) // 32 * 32
        if at is None:
            off = cur[0]
            cur[0] += nbytes
            assert cur[0] <= lim[0], (name, cur[0], lim[0])
        else:
            off = at
        return nc.alloc_sbuf_tensor_at(name, list(shape), dt, offset=off), off, nbytes

    actT, _, _ = alloc("actT", [128, 8, TOK], BF16)
    big2, _, _ = alloc("big2", [128, 8, TOK], BF16)
    wring = []
    for i in range(4):
        wt, _, _ = alloc(f"wring{i}", [128, 8, 512], BF16)
        wring.append(wt)
    identb, _, _ = alloc("identb", [128, 128], BF16)
    identf, _, _ = alloc("identf", [128, 128], F32)
    onesb, _, _ = alloc("onesb", [128, 128], BF16)
    ones512, _, _ = alloc("ones512", [128, 128], F32)
    ones128, _, _ = alloc("ones128", [128, 128], F32)
    epsln, _, _ = alloc("epsln", [128, 1], F32)
    lamt, _, _ = alloc("lamt", [128, 4, 64], F32)
    lamw, _, _ = alloc("lamw", [128, 8], F32)
    subg, _, _ = alloc("subg", [128, 2], F32)
    cvec, _, _ = alloc("cvec", [128, 3, 4], F32)
    convwT, _, _ = alloc("convwT", [128, 4, 32], F32)
    xin = []
    for i in range(2):
        t_, _, _ = alloc(f"xin{i}", [128, D], F32)
        xin.append(t_)
    xb16 = []
    for i in range(2):
        t_, _, _ = alloc(f"xb16_{i}", [128, D], BF16)
        xb16.append(t_)
    XOFF = cur[0]
    XSIZE = NT * D * 4
    cur[0] += XSIZE
    assert cur[0] <= SBTOP
    MOFF = cur[0]
    xres = nc.alloc_sbuf_tensor_at("xres", [128, NT, D], F32, offset=XOFF)
    o = XOFF
    vbf = nc.alloc_sbuf_tensor_at("vbf", [128, NT, 512], BF16, offset=o); o += NT * 512 * 2
    u_p = nc.alloc_sbuf_tensor_at("u_p", [128, 4, HIST + TP], BF16, offset=o); o += 4 * (HIST + TP) * 2 + 16
    o = (o + 31) // 32 * 32
    u_s = nc.alloc_sbuf_tensor_at("u_s", [128, 2, 4, HIST + TS + 2], BF16, offset=o); o += 2 * 4 * (HIST + TS + 2) * 2
    o = (o + 31) // 32 * 32
    XB_FREE = o
    assert XB_FREE + 31744 <= XOFF + XSIZE, (XB_FREE, XOFF + XSIZE)
    diag = nc.alloc_sbuf_tensor_at("diag", [128, 4, CW, 128], BF16, offset=XB_FREE)
    MSIZE = SBTOP - MOFF
    print("SBUF map: XOFF", XOFF, "MOFF", MOFF, "MSIZE", MSIZE)

    def malloc_reset(where="M"):
        if where == "M":
            cur[0] = MOFF; lim[0] = SBTOP
        elif where == "XU":
            cur[0] = XOFF + NT * 512 * 2; lim[0] = XOFF + XSIZE
        else:
            cur[0] = XB_FREE; lim[0] = XOFF + XSIZE

    psb = [nc.alloc_psum_tensor(f"ps{i}", [128, 512], F32) for i in range(8)]
    PB = fw.bufs("psum", 8)
    for b_ in PB:
        b_.excl = True
    rr = [0]

    def next_bank():
        i = rr[0] % 8
        rr[0] += 1
        return i

    actT_b = [[fw.buf(f"actT_{c}_{t}") for t in range(NT)] for c in range(8)]
    big2_b = [[fw.buf(f"big2_{c}_{t}") for t in range(NT)] for c in range(8)]
    wring_b = fw.bufs("wring", 4)
    const_b = fw.buf("const")
    par_b = fw.buf("params")
    xin_b = fw.bufs("xin", 2)
    xb16_b = fw.bufs("xb16", 2)
    xres_b = fw.bufs("xres", NT)
    xsc_b = fw.bufs("xsc", NT)
    vbf_b = fw.bufs("vbf", NT)
    u_b = [fw.bufs(f"u{j}_", 7) for j in range(4)]
    diag_b = fw.bufs("diag", 4)

    def cols_tiles(c0, n):
        res = []
        for t in range(NT):
            a = tcol(t)
            if a < c0 + n and a + trows(t) > c0:
                res.append(t)
        return res

    def k_const():
        fw.op("pool", lambda h: h.memset(identf[:], 0.0), writes=[const_b])
        fw.op("pool", lambda h: h.affine_select(out=identf[:], in_=identf[:], pattern=[[-1, 128]],
                                                 compare_op=ALU.not_equal, fill=1.0, base=0, channel_multiplier=1),
              reads=[const_b], writes=[const_b])
        fw.op("dve", lambda h: h.tensor_copy(identb[:], identf[:]), reads=[const_b], writes=[const_b])
        fw.op("dve", lambda h: h.memset(onesb[:], 1.0), writes=[const_b])
        fw.op("dve", lambda h: h.memset(ones512[:], 1.0 / 512.0), writes=[const_b])
        fw.op("dve", lambda h: h.memset(ones128[:], 1.0 / 128.0), writes=[const_b])
        fw.op("dve", lambda h: h.memset(epsln[:], LN_EPS), writes=[const_b])

    pieces = []
    for l in range(DEPTH):
        wi = win_d[l].rearrange("(kc p) n -> p kc n", p=128)
        pieces.append([(wi[:, :, 0:512], (slice(0, 8), slice(0, 512)))])
        pieces.append([(wi[:, :, 512:1024], (slice(0, 8), slice(0, 512)))])
        pieces.append([(wi[:, :, 1024:1536], (slice(0, 8), slice(0, 512)))])
        for half in range(2):
            pieces.append([(wi[:, :, 1536 + 256 * half:1536 + 256 * half + 256], (slice(0, 8), slice(0, 256))),
                           (wi[:, :, 2048 + 256 * half:2048 + 256 * half + 256], (slice(0, 8), slice(256, 512)))])
        wo = wout_d[l].rearrange("(kc p) n -> p kc n", p=128)
        pieces.append([(wo[:, :, 0:512], (slice(0, 8), slice(0, 512)))])
        pieces.append([(wo[:, :, 512:1024], (slice(0, 8), slice(0, 512)))])
        w1 = w1_d[l].rearrange("(kc p) n -> p kc n", p=128)
        w2 = w2_d[l].rearrange("(fc p) n -> p fc n", p=128)
        for g in range(8):
            pieces.append([(w1[:, :, g * 512:(g + 1) * 512], (slice(0, 8), slice(0, 512)))])
            pieces.append([(w2[:, 4 * g:4 * g + 4, :], "w2")])
    pstate = {"loaded": 0, "used": 0}

    def _load_next_piece():
        i = pstate["loaded"]
        if i >= len(pieces):
            return
        slot = i % 4
        srcs = pieces[i]
        wt = wring[slot]

        def fn(h, s, srcs=srcs, wt=wt):
            ins = None
            for (src, dst) in srcs:
                if dst == "w2":
                    o_ap = wt[:].rearrange("p a b -> p (a b)").rearrange("p (f n) -> p f n", f=4)
                    ins = h.dma_start(out=o_ap, in_=src).then_inc(s, 16)
                else:
                    ins = h.dma_start(out=wt[:, dst[0], dst[1]], in_=src).then_inc(s, 16)
            return ins
        extra = [xin_b[0], xin_b[1]] if (1 <= i <= 3) else []
        fw.dma("pool", fn, wring_b[slot], "w", reads=extra, writes=[wring_b[slot]], n=len(srcs))
        pstate["loaded"] += 1

    def acquire_piece():
        while pstate["loaded"] <= pstate["used"]:
            _load_next_piece()
        i = pstate["used"]
        pstate["used"] += 1
        return i % 4

    def prefetch_pieces(nheld=1):
        while pstate["loaded"] < min(len(pieces), pstate["used"] - nheld + 4):
            _load_next_piece()

    def make_xT_front(t, src_ap, src_bufs, par):
        rows = trows(t)
        xb = xb16[par]
        fw.op("act", lambda h: h.copy(xb[:rows, :], src_ap), reads=src_bufs, writes=[xb16_b[par]])

    def make_xT_back(t, par, evac="act"):
        rows = trows(t)
        c0 = tcol(t)
        xb = xb16[par]
        bi = next_bank()
        pst = psb[bi][:].bitcast(BF16)

        def tr(h):
            ins = None
            for c in range(8):
                ins = h.transpose(pst[:, c * 128:c * 128 + rows], xb[:rows, c * 128:(c + 1) * 128], identb[:rows, :rows])
            return ins
        fw.op("pe", tr, reads=[xb16_b[par], const_b], writes=[PB[bi]])
        src = pst.rearrange("p (c r) -> p c r", c=8)[:, :, :rows]
        if evac == "act":
            fw.op("act", lambda h: h.copy(actT[:, :, c0:c0 + rows], src),
                  reads=[PB[bi]], writes=[actT_b[c][tt] for c in range(8) for tt in cover(t)])
        else:
            fw.op("dve", lambda h: h.tensor_copy(actT[:, :, c0:c0 + rows], src),
                  reads=[PB[bi]], writes=[actT_b[c][tt] for c in range(8) for tt in cover(t)])

    def make_xT(t, src_ap, src_bufs, par):
        make_xT_front(t, src_ap, src_bufs, par)
        make_xT_back(t, par, evac="dve")

    def run_multi(items):
        n = len(items)
        K = max(len(it) for it in items)
        for step in range(n + K - 1):
            for k in range(K):
                i = step - k
                if 0 <= i < n and k < len(items[i]):
                    items[i][k]()

    def run_stages(stages, depth):
        n = len(stages)
        for i in range(n + depth):
            if i < n:
                stages[i][0]()
            if i >= depth:
                stages[i - depth][1]()

    def load_params(l):
        def fn(h, s):
            h.dma_start(out=lamt[:, 0, :], in_=lq1_d[l].partition_broadcast(128)).then_inc(s, 16)
            h.dma_start(out=lamt[:, 1, :], in_=lk1_d[l].partition_broadcast(128)).then_inc(s, 16)
            h.dma_start(out=lamt[:, 2, :], in_=lq2_d[l].partition_broadcast(128)).then_inc(s, 16)
            h.dma_start(out=lamt[:, 3, :], in_=lk2_d[l].partition_broadcast(128)).then_inc(s, 16)
            h.dma_start(out=subg[:, 0:1], in_=subg_d[l].rearrange("(p o) -> p o", o=1)).then_inc(s, 16)
            with nc.allow_non_contiguous_dma(reason="tiny per-channel vectors"):
                h.dma_start(out=cvec[:, 0, :], in_=convb_d[l].rearrange("(j p) -> p j", p=128)).then_inc(s, 16)
                h.dma_start(out=cvec[:, 1, :], in_=clng_d[l].rearrange("(j p) -> p j", p=128)).then_inc(s, 16)
                ins = h.dma_start(out=cvec[:, 2, :], in_=clnb_d[l].rearrange("(j p) -> p j", p=128)).then_inc(s, 16)
            return ins
        fw.dma("sp", fn, par_b, "w", writes=[par_b], n=8)
        fw.op("dve", lambda h: h.tensor_tensor(lamt[:, 0, :], lamt[:, 0, :], lamt[:, 1, :], ALU.mult), reads=[par_b], writes=[par_b])
        fw.op("dve", lambda h: h.tensor_tensor(lamt[:, 2, :], lamt[:, 2, :], lamt[:, 3, :], ALU.mult), reads=[par_b], writes=[par_b])
        fw.op("dve", lambda h: h.reduce_sum(lamw[:, 0:1], lamt[:, 0, :], AX.X), reads=[par_b], writes=[par_b])
        fw.op("dve", lambda h: h.reduce_sum(lamw[:, 1:2], lamt[:, 2, :], AX.X), reads=[par_b], writes=[par_b])
        fw.op("act", lambda h: h.activation(lamw[:, 2:4], lamw[:, 0:2], AF.Exp), reads=[par_b], writes=[par_b])
        fw.op("dve", lambda h: h.tensor_tensor(lamw[:, 4:5], lamw[:, 3:4], lamw[:, 2:3], ALU.subtract), reads=[par_b], writes=[par_b])
        fw.op("dve", lambda h: h.tensor_scalar(lamw[:, 5:6], lamw[:, 4:5], -lambda_init(l), None, ALU.add), reads=[par_b], writes=[par_b])
        fw.op("dve", lambda h: h.tensor_scalar(subg[:, 1:2], subg[:, 0:1], 1.0 - lambda_init(l), None, ALU.mult), reads=[par_b], writes=[par_b])

    def phase_A(l):
        malloc_reset("M")
        cosT, _, _ = alloc("cosT%d" % l, [128, NT, 32], F32)
        sinT, _, _ = alloc("sinT%d" % l, [128, NT, 32], F32)
        malloc_reset("X")
        rope_b = gbuf("rope")
        t1 = [alloc(f"t1_{l}_{i}", [128, 512], F32)[0] for i in range(2)]
        t2 = [alloc(f"t2_{l}_{i}", [128, 512], F32)[0] for i in range(2)]
        t1_b = gbufs("t1_", 2); t2_b = gbufs("t2_", 2)
        kf = [alloc(f"kf_{l}_{i}", [128, 512], F32)[0] for i in range(3)]
        kf_b = gbufs("kf", 3)
        vf = [alloc(f"vf_{l}_{i}", [128, 512], F32)[0] for i in range(2)]
        vf_b = gbufs("vf", 2)
        qb = [alloc(f"qb_{l}_{i}", [128, 512], BF16)[0] for i in range(3)]
        qb_b = gbufs("qb", 3)
        sig = [alloc(f"sig_{l}_{i}", [128, 512], F32)[0] for i in range(2)]
        sig_b = gbufs("sig", 2)
        utail, _, _ = alloc(f"utail{l}", [128, 4, 3, 32], F32)
        utail_b = [[gbuf(f"utail{j}_{q}") for q in range(3)] for j in range(4)]
        ctail, _, _ = alloc(f"ctail{l}", [32, 512], F32)
        ctail_b = gbuf("ctail")
        chist, _, _ = alloc(f"chist{l}", [32, 2, 512], F32)
        chist_b = gbuf("chist")
        cwst, _, _ = alloc(f"cwst{l}", [32, 512], F32)
        cwst_b = gbuf("cwst")

        fw.dma("sp", lambda h, s: (h.dma_start(out=cosT[:], in_=cos_d[:, :, :]).then_inc(s, 16),
                                   h.dma_start(out=sinT[:], in_=sin_d[:, :, :]).then_inc(s, 16))[1],
               rope_b, "w", writes=[rope_b], n=2)
        fw.dma("sp", lambda h, s: (h.dma_start(out=chist[:HIST, 0, :], in_=cc_d[l, 0]).then_inc(s, 16),
                                   h.dma_start(out=chist[:HIST, 1, :], in_=cc_d[l, 1]).then_inc(s, 16))[1],
               chist_b, "w", writes=[chist_b], n=2)
        fw.op("pool", lambda h: h.memset(cwst[:, :], 0.0), writes=[cwst_b])
        fw.dma("sp", lambda h, s: h.dma_start(out=cwst[:CW, :], in_=convw_d[l]).then_inc(s, 16),
               cwst_b, "w", writes=[cwst_b], n=1)
        fw.op("pool", lambda h: h.memset(u_p[:, :, 0:HIST], 0.0), writes=[u_b[j][6] for j in range(4)])
        if STOP_AFTER == "A0pre1":
            return
        for s_ in range(2):
            bi = next_bank()

            def trh(h, s_=s_, bi=bi):
                ins = None
                for j in range(4):
                    ins = h.transpose(psb[bi][:, j * 32:j * 32 + HIST], chist[:HIST, s_, j * 128:(j + 1) * 128], identf[:HIST, :HIST])
                return ins
            fw.op("pe", trh, reads=[chist_b, const_b], writes=[PB[bi]])
            fw.op("act", lambda h, s_=s_, bi=bi: h.copy(u_s[:, s_, :, 0:HIST], psb[bi][:, 0:128].rearrange("p (j c) -> p j c", j=4)[:, :, 0:HIST]),
                  reads=[PB[bi]], writes=[u_b[j][6] for j in range(4)])
        if STOP_AFTER == "A0pre2":
            return
        bi = next_bank()

        def trw(h, bi=bi):
            ins = None
            for j in range(4):
                ins = h.transpose(psb[bi][:, j * 32:j * 32 + 32], cwst[:32, j * 128:(j + 1) * 128], identf[:32, :32])
            return ins
        fw.op("pe", trw, reads=[cwst_b, const_b], writes=[PB[bi]])
        fw.op("act", lambda h, bi=bi: h.copy(convwT[:, :, :], psb[bi][:, 0:128].rearrange("p (j c) -> p j c", j=4)),
              reads=[PB[bi]], writes=[par_b])

        def tokmajor_mm(t, slot):
            rows = trows(t); c0 = tcol(t)
            bi = next_bank()
            wt = wring[slot]

            def mm(h):
                ins = None
                for kc in range(8):
                    ins = h.matmul(psb[bi][:rows, :], actT[:, kc, c0:c0 + rows], wt[:, kc, :], start=(kc == 0), stop=(kc == 7))
                return ins
            fw.op("pe", mm, reads=[actT_b[c][tt] for c in range(8) for tt in cover(t)] + [wring_b[slot]], writes=[PB[bi]])
            return bi

        def rope_mul(t, bi, par):
            rows = trows(t)
            ps4 = psb[bi][:rows, :].rearrange("p (h two i) -> p h two i", two=2, i=32)
            ps3 = psb[bi][:rows, :].rearrange("p (g i) -> p g i", i=32)
            cb16 = cosT[:rows, t, :].unsqueeze(1).to_broadcast([rows, 16, 32])
            sb8 = sinT[:rows, t, :].unsqueeze(1).to_broadcast([rows, 8, 32])
            a = t1[par]; b = t2[par]
            a3 = a[:rows, :].rearrange("p (g i) -> p g i", i=32)
            b4 = b[:rows, :].rearrange("p (h two i) -> p h two i", two=2, i=32)
            fw.op("dve", lambda h: h.tensor_tensor(a3, ps3, cb16, ALU.mult), reads=[PB[bi], rope_b], writes=[t1_b[par]])
            fw.op("dve", lambda h: h.tensor_tensor(b4[:, :, 0, :], ps4[:, :, 1, :], sb8, ALU.mult), reads=[PB[bi], rope_b], writes=[t2_b[par]])
            fw.op("dve", lambda h: h.tensor_tensor(b4[:, :, 1, :], ps4[:, :, 0, :], sb8, ALU.mult), reads=[PB[bi], rope_b], writes=[t2_b[par]])

        def rope_comb(t, out_ap, out_bufs, par):
            rows = trows(t)
            a = t1[par]; b = t2[par]
            a4 = a[:rows, :].rearrange("p (h two i) -> p h two i", two=2, i=32)
            b4 = b[:rows, :].rearrange("p (h two i) -> p h two i", two=2, i=32)
            o4 = out_ap.rearrange("p (h two i) -> p h two i", two=2, i=32)
            fw.op("pool", lambda h: h.tensor_tensor(o4[:, :, 0, :], a4[:, :, 0, :], b4[:, :, 0, :], ALU.subtract),
                  reads=[t1_b[par], t2_b[par]], writes=out_bufs)
            fw.op("pool", lambda h: h.tensor_tensor(o4[:, :, 1, :], a4[:, :, 1, :], b4[:, :, 1, :], ALU.add),
                  reads=[t1_b[par], t2_b[par]], writes=out_bufs)

        def to_featmajor_tr(t, src_bf, src_buf, st):
            rows = trows(t)
            bi = next_bank()
            st["bi"] = bi
            pst = psb[bi][:].bitcast(BF16)

            def tr(h):
                ins = None
                for c in range(4):
                    ins = h.transpose(pst[:, c * 128:c * 128 + rows], src_bf[:rows, c * 128:(c + 1) * 128], identb[:rows, :rows])
                return ins
            fw.op("pe", tr, reads=[src_buf, const_b], writes=[PB[bi]])

        def to_featmajor_ev(t, dst, dst_bufs, st):
            rows = trows(t); c0 = tcol(t)
            bi = st["bi"]
            pst = psb[bi][:].bitcast(BF16)
            src = pst[:, 0:512].rearrange("p (c r) -> p c r", c=4)[:, :, :rows]
            fw.op("act", lambda h: h.copy(dst[:, :, c0:c0 + rows], src), reads=[PB[bi]], writes=dst_bufs)

        def to_featmajor(t, src_bf, src_buf, dst, dst_bufs_fn):
            rows = trows(t); c0 = tcol(t)
            bi = next_bank()
            pst = psb[bi][:].bitcast(BF16)

            def tr(h):
                ins = None
                for c in range(4):
                    ins = h.transpose(pst[:, c * 128:c * 128 + rows], src_bf[:rows, c * 128:(c + 1) * 128], identb[:rows, :rows])
                return ins
            fw.op("pe", tr, reads=[src_buf, const_b], writes=[PB[bi]])
            src = pst[:, 0:512].rearrange("p (c r) -> p c r", c=4)[:, :, :rows]
            fw.op("act", lambda h: h.copy(dst[:, :, c0:c0 + rows], src), reads=[PB[bi]], writes=dst_bufs_fn(t))

        if STOP_AFTER == "A0pre":
            return
        slot = acquire_piece(); prefetch_pieces()
        items = []
        for t in range(NMT):
            st = {}

            def a0(t=t, slot=slot, st=st):
                st["mm"] = tokmajor_mm(t, slot)

            def a1(t=t, st=st):
                rope_mul(t, st["mm"], t % 2)

            def a2(t=t):
                rows = trows(t); r3 = t % 3
                rope_comb(t, qb[r3][:rows, :], [qb_b[r3]], t % 2)

            def a3(t=t, st=st):
                r3 = t % 3
                to_featmajor_tr(t, qb[r3], qb_b[r3], st)

            def a4(t=t, st=st):
                to_featmajor_ev(t, big2[:, 0:4, :], [big2_b[c][tt] for c in range(4) for tt in cover(t)], st)
            items.append([a0, a1, a2, a3, a4])
        kslot = {}
        for t in range(NMT):
            st = {}
            ix = NMT + t

            def a0(t=t, st=st):
                if "s" not in kslot:
                    kslot["s"] = acquire_piece(); prefetch_pieces()
                st["mm"] = tokmajor_mm(t, kslot["s"])

            def a1(t=t, st=st, ix=ix):
                rope_mul(t, st["mm"], ix % 2)

            def a2(t=t, ix=ix):
                rows = trows(t); r3 = ix % 3
                rope_comb(t, kf[r3][:rows, :], [kf_b[r3]], ix % 2)

            def a2b(t=t, ix=ix):
                rows = trows(t); r3 = ix % 3; c0 = tcol(t)
                fw.dma("sp", lambda h, s: h.dma_start(out=nk_d[l, c0:c0 + rows, :], in_=kf[r3][:rows, :]).then_inc(s, 16),
                       kf_b[r3], "r", reads=[kf_b[r3]])
                fw.op("act", lambda h: h.copy(qb[r3][:rows, :], kf[r3][:rows, :]), reads=[kf_b[r3]], writes=[qb_b[r3]])

            def a3(t=t, st=st, ix=ix):
                r3 = ix % 3
                to_featmajor_tr(t, qb[r3], qb_b[r3], st)

            def a4(t=t, st=st):
                to_featmajor_ev(t, big2[:, 4:8, :], [big2_b[4 + c][tt] for c in range(4) for tt in cover(t)], st)
            items.append([a0, a1, a2, a2b, a3, a4])
        run_multi(items)
        if STOP_AFTER == "A0k":
            return
        slot = acquire_piece(); prefetch_pieces()
        _GEOM["merged"] = False
        for t in range(NT):
            rows = trows(t); par = t % 2; c0 = tcol(t)
            bi = tokmajor_mm(t, slot)
            fw.op("act", lambda h, par=par, rows=rows, bi=bi: h.copy(vf[par][:rows, :], psb[bi][:rows, :]), reads=[PB[bi]], writes=[vf_b[par]])
            fw.op("dve", lambda h, t=t, rows=rows, bi=bi: h.tensor_copy(vbf[:rows, t, :], psb[bi][:rows, :]), reads=[PB[bi]], writes=[vbf_b[t]])
            fw.dma("sp", lambda h, s, par=par, rows=rows, c0=c0: h.dma_start(out=nv_d[l, c0:c0 + rows, :], in_=vf[par][:rows, :]).then_inc(s, 16),
                   vf_b[par], "r", reads=[vf_b[par]])
        _GEOM["merged"] = True
        if STOP_AFTER == "A0v":
            return
        glu_deferred = []
        for half in range(2):
            slot = acquire_piece(); prefetch_pieces()
            wt = wring[slot]
            for jj in range(2):
                j = 2 * half + jj
                for bi_, (c0, n) in enumerate(BLOCKS):
                    tl = block_tiles(bi_)
                    ba = next_bank(); bg = next_bank()

                    def mma(h, ba=ba, jj=jj, c0=c0, n=n, wt=wt):
                        ins = None
                        for kc in range(8):
                            ins = h.matmul(psb[ba][:, :n], wt[:, kc, jj * 128:(jj + 1) * 128], actT[:, kc, c0:c0 + n], start=(kc == 0), stop=(kc == 7))
                        return ins

                    def mmg(h, bg=bg, jj=jj, c0=c0, n=n, wt=wt):
                        ins = None
                        for kc in range(8):
                            ins = h.matmul(psb[bg][:, :n], wt[:, kc, 256 + jj * 128:256 + (jj + 1) * 128], actT[:, kc, c0:c0 + n], start=(kc == 0), stop=(kc == 7))
                        return ins
                    rd = [actT_b[c][t] for c in range(8) for t in tl] + [wring_b[slot]]
                    fw.op("pe", mma, reads=rd, writes=[PB[ba]])
                    fw.op("pe", mmg, reads=rd, writes=[PB[bg]])
                    par = bi_ % 2
                    fw.op("act", lambda h, par=par, bg=bg, n=n: h.activation(sig[par][:, :n], psb[bg][:, :n], AF.Sigmoid),
                          reads=[PB[bg]], writes=[sig_b[par]])
                    for fn_ in glu_deferred:
                        fn_()
                    del glu_deferred[:]
                    if bi_ < 4:
                        fw.op("dve", lambda h, par=par, ba=ba, j=j, c0=c0: h.tensor_tensor(u_p[:, j, HIST + c0:HIST + c0 + 512], psb[ba][:, :512], sig[par][:, :512], ALU.mult),
                              reads=[PB[ba], sig_b[par]], writes=[u_b[j][bi_]])
                        if bi_ == 3:
                            fw.op("dve", lambda h, par=par, ba=ba, j=j: h.tensor_tensor(utail[:, j, 0, 0:HIST], psb[ba][:, 512 - HIST:512], sig[par][:, 512 - HIST:512], ALU.mult),
                                  reads=[PB[ba], sig_b[par]], writes=[utail_b[j][0]])
                    else:
                        for s_ in range(2):
                            fw.op("dve", lambda h, par=par, ba=ba, j=j, s_=s_: h.tensor_tensor(u_s[:, s_, j, HIST:HIST + TS], psb[ba][:, s_ * TS:(s_ + 1) * TS], sig[par][:, s_ * TS:(s_ + 1) * TS], ALU.mult),
                                  reads=[PB[ba], sig_b[par]], writes=[u_b[j][4 + s_]])
                            fw.op("dve", lambda h, par=par, ba=ba, j=j, s_=s_: h.tensor_tensor(utail[:, j, 1 + s_, 0:HIST], psb[ba][:, s_ * TS + TS - HIST:(s_ + 1) * TS], sig[par][:, s_ * TS + TS - HIST:(s_ + 1) * TS], ALU.mult),
                                  reads=[PB[ba], sig_b[par]], writes=[utail_b[j][1 + s_]])
                    if bi_ >= 3:
                        seqs = [0] if bi_ == 3 else [1, 2]
                        for sq in seqs:
                            def tail_out(sq=sq, j=j):
                                bt = next_bank()
                                fw.op("pe", lambda h: h.transpose(psb[bt][:HIST, 0:128], utail[:, j, sq, 0:HIST], identf[:, :]),
                                      reads=[utail_b[j][sq], const_b], writes=[PB[bt]])
                                fw.op("act", lambda h: h.copy(ctail[:HIST, j * 128:(j + 1) * 128], psb[bt][:HIST, 0:128]),
                                      reads=[PB[bt]], writes=[ctail_b])
                                fw.dma("sp", lambda h, s: h.dma_start(out=ncv_d[l, sq, :, j * 128:(j + 1) * 128], in_=ctail[:HIST, j * 128:(j + 1) * 128]).then_inc(s, 16),
                                       ctail_b, "r", reads=[ctail_b])
                            glu_deferred.append(tail_out)
        for fn_ in glu_deferred:
            fn_()
        del glu_deferred[:]

    _bufcache = {}

    def gbuf(name):
        if name not in _bufcache:
            _bufcache[name] = fw.buf(name)
        return _bufcache[name]

    def gbufs(name, n):
        return [gbuf(f"{name}{i}") for i in range(n)]

    def phase_B1(l):
        malloc_reset("M")
        csb = [alloc(f"csb{l}_{j}", [128, 512], F32)[0] for j in range(4)]
        csb_b = gbufs("csb", 4)
        csq = [alloc(f"csq{l}_{i}", [128, 512], F32)[0] for i in range(2)]
        csq_b = gbufs("csq", 2)
        mean, _, _ = alloc(f"cmean{l}", [128, 512], F32); mean_b = gbuf("cmean")
        rstd, _, _ = alloc(f"crstd{l}", [128, 512], F32); rstd_b = gbuf("crstd")
        zz = [alloc(f"cz{l}_{i}", [128, 512], F32)[0] for i in range(2)]
        zz_b = gbufs("cz", 2)
        dg_b = [[gbuf(f"dg{j}_{tap}") for tap in range(CW)] for j in range(4)]
        for j in range(4):
            for tap in range(CW):
                if tap % 2 == 0:
                    fw.op("dve", lambda h, j=j, tap=tap: h.tensor_scalar(diag[:, j, tap, :], identf[:, :], convwT[:, j, tap:tap + 1], None, ALU.mult),
                          reads=[const_b, par_b], writes=[dg_b[j][tap]])
                else:
                    fw.op("act", lambda h, j=j, tap=tap: h.activation(diag[:, j, tap, :], identf[:, :], AF.Copy, scale=convwT[:, j, tap:tap + 1]),
                          reads=[const_b, par_b], writes=[dg_b[j][tap]])
        seqs = []
        for bi_ in range(4):
            seqs.append(("p", bi_, 512, bi_ * 512))
        seqs.append(("s", 0, 2 * TS, TP))
        for (kind, idx, n, oc0) in seqs:
            banks = []
            for j in range(4):
                bk = next_bank(); banks.append(bk)
                if kind == "p":
                    src = lambda tap, j=j, idx=idx: u_p[:, j, idx * 512 + tap: idx * 512 + tap + 512]
                    rd = [u_b[j][idx], u_b[j][idx - 1] if idx > 0 else u_b[j][6]]
                else:
                    src = lambda tap, j=j: u_s[:, :, j, tap: tap + TS]
                    rd = [u_b[j][4], u_b[j][5], u_b[j][6]]

                def cv(h, bk=bk, j=j, src=src, n=n, kind=kind):
                    ins = None
                    for tap in range(CW):
                        o_ap = psb[bk][:, :n] if kind == "p" else psb[bk][:, :n].rearrange("p (a b) -> p a b", a=2)
                        ins = h.matmul(o_ap, diag[:, j, tap, :], src(tap), start=(tap == 0), stop=(tap == CW - 1))
                    return ins
                fw.op("pe", cv, reads=rd + dg_b[j], writes=[PB[bk]])
            bm = next_bank(); bq = next_bank()
            for j in range(4):
                bk = banks[j]
                fw.op("act", lambda h, j=j, bk=bk, n=n: h.activation(csb[j][:, :n], psb[bk][:, :n], AF.Identity, bias=cvec[:, 0, j:j + 1], scale=1.0),
                      reads=[PB[bk], par_b], writes=[csb_b[j]])
                fw.op("act", lambda h, j=j, bk=bk, n=n: h.activation(csq[j % 2][:, :n], psb[bk][:, :n], AF.Square, bias=cvec[:, 0, j:j + 1], scale=1.0),
                      reads=[PB[bk], par_b], writes=[csq_b[j % 2]])
                fw.op("pe", lambda h, j=j, n=n, bm=bm: h.matmul(psb[bm][:, :n], ones512[:, :], csb[j][:, :n], start=(j == 0), stop=(j == 3)),
                      reads=[csb_b[j], const_b], writes=[PB[bm]])
                fw.op("pe", lambda h, j=j, n=n, bq=bq: h.matmul(psb[bq][:, :n], ones512[:, :], csq[j % 2][:, :n], start=(j == 0), stop=(j == 3)),
                      reads=[csq_b[j % 2], const_b], writes=[PB[bq]])
            fw.op("act", lambda h, n=n, bm=bm: h.copy(mean[:, :n], psb[bm][:, :n]), reads=[PB[bm]], writes=[mean_b])
            fw.op("dve", lambda h, n=n: h.tensor_tensor(rstd[:, :n], mean[:, :n], mean[:, :n], ALU.mult), reads=[mean_b], writes=[rstd_b])
            fw.op("dve", lambda h, n=n, bq=bq: h.tensor_tensor(rstd[:, :n], psb[bq][:, :n], rstd[:, :n], ALU.subtract), reads=[PB[bq], rstd_b], writes=[rstd_b])
            fw.op("act", lambda h, n=n: h.activation(rstd[:, :n], rstd[:, :n], AF.Ln, bias=epsln[:, 0:1], scale=1.0), reads=[rstd_b, const_b], writes=[rstd_b])
            fw.op("act", lambda h, n=n: h.activation(rstd[:, :n], rstd[:, :n], AF.Exp, scale=-0.5), reads=[rstd_b], writes=[rstd_b])
            otl = cols_tiles(oc0, n)
            for j in range(4):
                zi = j % 2
                fw.op("dve", lambda h, j=j, zi=zi, n=n: h.tensor_tensor(zz[zi][:, :n], csb[j][:, :n], mean[:, :n], ALU.subtract),
                      reads=[csb_b[j], mean_b], writes=[zz_b[zi]])
                fw.op("dve", lambda h, zi=zi, n=n: h.tensor_tensor(zz[zi][:, :n], zz[zi][:, :n], rstd[:, :n], ALU.mult),
                      reads=[zz_b[zi], rstd_b], writes=[zz_b[zi]])
                fw.op("act", lambda h, j=j, zi=zi, n=n, oc0=oc0: h.activation(actT[:, 4 + j, oc0:oc0 + n], zz[zi][:, :n], AF.Silu, bias=cvec[:, 2, j:j + 1], scale=cvec[:, 1, j:j + 1]),
                      reads=[zz_b[zi], par_b], writes=[actT_b[4 + j][t] for t in otl])

    ST_BANKS = [0, 1, 2, 7]
    O_BANKS = [3, 4]
    S_BANKS = [5, 6]

    def phase_B2(l):
        malloc_reset("XU")
        NKV = 4
        Kst = [alloc(f"Kst{l}_{i}", [128, 16, 128], BF16)[0] for i in range(NKV)]; Kst_b = gbufs("Kst", NKV)
        Vst = [alloc(f"Vst{l}_{i}", [128, 16, 128], BF16)[0] for i in range(NKV)]; Vst_b = gbufs("Vst", NKV)
        KT = [alloc(f"KT{l}_{i}", [128, 2048], BF16)[0] for i in range(2)]; KT_b = gbufs("KT", 2)
        PT = [alloc(f"PT{l}_{i}", [128, 512], BF16)[0] for i in range(4)]; PT_b = gbufs("PT", 4)
        malloc_reset("M")
        r1, _, _ = alloc(f"r1_{l}", [128, 512], F32); r2, _, _ = alloc(f"r2_{l}", [128, 512], F32)
        ta, _, _ = alloc(f"ta_{l}", [128, 512], F32); tb, _, _ = alloc(f"tb_{l}", [128, 512], F32)
        sq, _, _ = alloc(f"sq_{l}", [128, 512], F32); rr, _, _ = alloc(f"rr_{l}", [128, 512], F32)
        r1_b = gbuf("r1"); r2_b = gbuf("r2"); ta_b = gbuf("ta"); tb_b = gbuf("tb"); sq_b = gbuf("sq"); rr_b = gbuf("rr")
        stc = [0]; ptc = [0]

        pipe = []
        deferred = []

        def attend(j, qc0, nq, keytiles, q_reads, out_bufs):
            nkt = len(keytiles)
            for sub in range(2):
                bO = O_BANKS[sub]; bS = S_BANKS[sub]
                ps_ = slice(sub * 64, (sub + 1) * 64)
                per_bank = (512 // nq) if nq <= 64 else 1
                groups = []
                cur_g = []
                for kt_ in keytiles:
                    if cur_g and (len(cur_g) >= per_bank or kt_["nk"] != cur_g[0]["nk"]):
                        groups.append(cur_g); cur_g = []
                    cur_g.append(kt_)
                if cur_g:
                    groups.append(cur_g)
                done = 0
                for gi, g in enumerate(groups):
                    is_last = (sub == 1 and gi == len(groups) - 1)

                    def front(g=g, sub=sub, ps_=ps_, st={}):
                        bST = ST_BANKS[stc[0] % 4]; stc[0] += 1
                        pi = ptc[0] % 4; ptc[0] += 1
                        st["pi"] = pi
                        nk = g[0]["nk"]

                        def qk(h):
                            ins = None
                            for i, kt_ in enumerate(g):
                                c0 = kt_["c0"]
                                ins = h.matmul(psb[bST][:nk, i * nq + c0:(i + 1) * nq], kt_["kT"](sub), big2[ps_, j, qc0 + c0:qc0 + nq], start=True, stop=True)
                            return ins
                        rds = list(q_reads)
                        for kt_ in g:
                            rds += kt_["kreads"]
                        fw.op("pe", qk, reads=rds, writes=[PB[bST]])
                        lo = g[0]["c0"]; hi = len(g) * nq
                        fw.op("act", lambda h: h.activation(PT[pi][:nk, lo:hi], psb[bST][:nk, lo:hi], AF.Exp, scale=0.125),
                              reads=[PB[bST]], writes=[PT_b[pi]])
                        if g[0]["diag"]:
                            fw.op("dve", lambda h: h.memset(PT[pi][64:128, lo:lo + 64], 0.0), writes=[PT_b[pi]])

                    def back(g=g, done=done, bO=bO, bS=bS, is_last=is_last, st=None):
                        pass
                    st_ = {}
                    front_fn = (lambda f=front, st_=st_: f(st=st_))

                    def back_fn(g=g, done=done, bO=bO, bS=bS, is_last=is_last, st_=st_):
                        pi = st_["pi"]
                        nk = g[0]["nk"]

                        def av(h):
                            ins = None
                            for i, kt_ in enumerate(g):
                                c0 = kt_["c0"]
                                first = (done + i == 0); last = (done + i == nkt - 1)
                                h.matmul(psb[bO][:, c0:nq], kt_["v"], PT[pi][:nk, i * nq + c0:(i + 1) * nq], start=first, stop=last)
                                ins = h.matmul(psb[bS][:, c0:nq], onesb[:nk, :], PT[pi][:nk, i * nq + c0:(i + 1) * nq], start=first, stop=last)
                            return ins
                        rds = [PT_b[pi], const_b]
                        for kt_ in g:
                            rds += kt_["vreads"]
                        fw.op("pe", av, reads=rds, writes=[PB[bO], PB[bS]])
                        if is_last:
                            normalize(j, qc0, nq, out_bufs)
                    pipe.append((front_fn, back_fn))
                    done += len(g)

        def normalize(j, qc0, n, out_bufs):
            bO0, bO1 = O_BANKS; bS0, bS1 = S_BANKS
            fw.op("act", lambda h: h.activation(r1[:, :n], psb[bS0][:, :n], AF.Ln), reads=[PB[bS0]], writes=[r1_b])
            fw.op("act", lambda h: h.activation(r2[:, :n], psb[bS1][:, :n], AF.Ln), reads=[PB[bS1]], writes=[r2_b])
            fw.op("dve", lambda h: h.tensor_copy(ta[:, :n], psb[bO0][:, :n]), reads=[PB[bO0]], writes=[ta_b])
            fw.op("dve", lambda h: h.tensor_copy(tb[:, :n], psb[bO1][:, :n]), reads=[PB[bO1]], writes=[tb_b])
            fw.op("act", lambda h: h.activation(r1[:, :n], r1[:, :n], AF.Exp, scale=-1.0), reads=[r1_b], writes=[r1_b])
            fw.op("act", lambda h: h.activation(r2[:, :n], r2[:, :n], AF.Exp, scale=-1.0), reads=[r2_b], writes=[r2_b])
            fw.op("dve", lambda h: h.tensor_tensor(ta[:, :n], ta[:, :n], r1[:, :n], ALU.mult), reads=[ta_b, r1_b], writes=[ta_b])
            fw.op("dve", lambda h: h.tensor_tensor(tb[:, :n], tb[:, :n], r2[:, :n], ALU.mult), reads=[tb_b, r2_b], writes=[tb_b])
            fw.op("dve", lambda h: h.scalar_tensor_tensor(ta[:, :n], tb[:, :n], lamw[:, 5:6], ta[:, :n], ALU.mult, ALU.add),
                  reads=[tb_b, ta_b, par_b], writes=[ta_b])
            fw.op("act", lambda h: h.activation(sq[:, :n], ta[:, :n], AF.Square), reads=[ta_b], writes=[sq_b])
            deferred.append([4, lambda: normalize2(j, qc0, n, out_bufs)])

        def normalize2(j, qc0, n, out_bufs):
            R_BANK = ST_BANKS[stc[0] % 4]; stc[0] += 1
            fw.op("pe", lambda h: h.matmul(psb[R_BANK][:, :n], ones128[:, :], sq[:, :n], start=True, stop=True), reads=[sq_b, const_b], writes=[PB[R_BANK]])
            fw.op("act", lambda h: h.activation(rr[:, :n], psb[R_BANK][:, :n], AF.Ln, bias=epsln[:, 0:1], scale=1.0), reads=[PB[R_BANK], const_b], writes=[rr_b])
            fw.op("act", lambda h: h.activation(rr[:, :n], rr[:, :n], AF.Exp, scale=-0.5), reads=[rr_b], writes=[rr_b])
            fw.op("dve", lambda h: h.tensor_tensor(ta[:, :n], ta[:, :n], rr[:, :n], ALU.mult), reads=[ta_b, rr_b], writes=[ta_b])
            fw.op("dve", lambda h: h.tensor_scalar(actT[:, j, qc0:qc0 + n], ta[:, :n], subg[:, 1:2], None, ALU.mult),
                  reads=[ta_b, par_b], writes=out_bufs)

        def run_pipe(depth=3):
            n = len(pipe)
            for i in range(n + depth):
                if i < n:
                    pipe[i][0]()
                if i >= depth:
                    pipe[i - depth][1]()
                for d in list(deferred):
                    d[0] -= 1
                    if d[0] <= 0:
                        deferred.remove(d)
                        d[1]()
            for d in list(deferred):
                deferred.remove(d)
                d[1]()
            del pipe[:]

        combos = [(s_, j) for s_ in range(2) for j in range(4)]

        def sample_load(ci):
            s_, j = combos[ci]; par = ci % NKV
            fw.dma("pool", lambda h, s: h.dma_start(out=Kst[par][:], in_=ck_d[l, s_, :, j * 128:(j + 1) * 128].rearrange("(kt p) c -> p kt c", p=128)).then_inc(s, 16),
                   Kst_b[par], "w", writes=[Kst_b[par]])
            fw.dma("pool", lambda h, s: h.dma_start(out=Vst[par][:], in_=cv_d[l, s_, :, j * 128:(j + 1) * 128].rearrange("(kt p) c -> p kt c", p=128)).then_inc(s, 16),
                   Vst_b[par], "w", writes=[Vst_b[par]])

        def sample_prep(ci):
            s_, j = combos[ci]; par = ci % 2; kv = ci % NKV
            for hh in range(2):
                bk = next_bank()
                pst = psb[bk][:].bitcast(BF16)

                def trk(h, hh=hh, pst=pst):
                    ins = None
                    for i in range(8):
                        ins = h.transpose(pst[:, i * 128:(i + 1) * 128], Kst[kv][:, hh * 8 + i, :], identb[:, :])
                    return ins
                fw.op("pe", trk, reads=[Kst_b[kv], const_b], writes=[PB[bk]])
                fw.op("dve", lambda h, hh=hh, pst=pst: h.tensor_copy(KT[par][:, hh * 1024:(hh + 1) * 1024], pst[:, :]),
                      reads=[PB[bk]], writes=[KT_b[par]])

        def sample_attend(ci):
            s_, j = combos[ci]; par = ci % 2; kv = ci % NKV
            keytiles = []
            for kt in range(16):
                keytiles.append(dict(kT=(lambda sub, kt=kt: KT[par][sub * 64:(sub + 1) * 64, kt * 128:(kt + 1) * 128]),
                                     v=Vst[kv][:, kt, :], nk=128, c0=0, diag=False, kreads=[KT_b[par]], vreads=[Vst_b[kv]]))
            nc0 = TP + s_ * TS
            keytiles.append(dict(kT=(lambda sub: big2[sub * 64:(sub + 1) * 64, 4 + j, nc0:nc0 + TS]),
                                 v=vbf[:TS, 16 + s_, j * 128:(j + 1) * 128], nk=TS, c0=0, diag=False,
                                 kreads=[big2_b[4 + j][16 + s_]], vreads=[vbf_b[16 + s_]]))
            attend(j, nc0, TS, keytiles, [big2_b[j][16 + s_]], [actT_b[j][16 + s_]])
            run_pipe()

        for ci in range(NKV):
            sample_load(ci)
        for j in range(4):
            for qb in range(4):
                keytiles = []
                for kt in range(4 * qb + 4):
                    c0 = max(0, kt * 128 - qb * 512)
                    keytiles.append(dict(kT=(lambda sub, j=j, kt=kt: big2[sub * 64:(sub + 1) * 64, 4 + j, kt * 128:(kt + 1) * 128]),
                                         v=vbf[:, kt, j * 128:(j + 1) * 128], nk=128, c0=c0, diag=(kt >= 4 * qb),
                                         kreads=[big2_b[4 + j][kt]], vreads=[vbf_b[kt]]))
                attend(j, qb * 512, 512, keytiles, [big2_b[j][4 * qb + i] for i in range(4)], [actT_b[j][4 * qb + i] for i in range(4)])
        run_pipe()

        sample_prep(0)
        for ci in range(len(combos)):
            if ci + 1 < len(combos):
                sample_prep(ci + 1)
            sample_attend(ci)
            if ci + NKV < len(combos):
                sample_load(ci + NKV)

    ln_state = {}

    def ln_setup():
        malloc_reset("M")
        gB, _, _ = alloc("lngB", [128, D], F32); bB, _, _ = alloc("lnbB", [128, D], F32)
        st = [alloc(f"lnst{i}", [128, 2, 6], F32)[0] for i in range(6)]
        mv = [alloc(f"lnmv{i}", [128, 4], F32)[0] for i in range(6)]
        rl = [alloc(f"rl{i}", [128, 512], F32)[0] for i in range(2)]
        ln_state.update(gB=gB, bB=bB, st=st, mv=mv, rl=rl, lnp_b=gbuf("lnp"), st_b=gbufs("lnst", 6), mv_b=gbufs("lnmv", 6), rl_b=gbufs("rl", 2))

    def ln_load(g_d, b_d):
        L = ln_state
        fw.dma("sp", lambda h, s: (h.dma_start(out=L["gB"][:], in_=g_d.partition_broadcast(128)).then_inc(s, 16),
                                   h.dma_start(out=L["bB"][:], in_=b_d.partition_broadcast(128)).then_inc(s, 16))[1],
               L["lnp_b"], "w", writes=[L["lnp_b"]], n=2)

    def ln_stats(t, par):
        L = ln_state
        rows = trows(t)
        st = L["st"][par]; mv = L["mv"][par]; st_b = L["st_b"][par]; mv_b = L["mv_b"][par]
        xb_ = xres_b[t]
        fw.op("dve", lambda h: h.bn_stats(st[:rows, 0, :], xres[:rows, t, 0:512]), reads=[xb_], writes=[st_b])
        fw.op("dve", lambda h: h.bn_stats(st[:rows, 1, :], xres[:rows, t, 512:1024]), reads=[xb_], writes=[st_b])
        fw.op("dve", lambda h: h.bn_aggr(mv[:rows, 0:2], st[:rows].rearrange("p a b -> p (a b)")), reads=[st_b], writes=[mv_b])

    def ln_rstd(t, par):
        L = ln_state
        rows = trows(t)
        mv = L["mv"][par]; mv_b = L["mv_b"][par]
        fw.op("act", lambda h: h.activation(mv[:rows, 2:3], mv[:rows, 1:2], AF.Ln, bias=epsln[:rows, 0:1], scale=1.0), reads=[mv_b, const_b], writes=[mv_b])
        fw.op("act", lambda h: h.activation(mv[:rows, 3:4], mv[:rows, 2:3], AF.Exp, scale=-0.5), reads=[mv_b], writes=[mv_b])

    def ln_nmr(t, par):
        L = ln_state
        rows = trows(t)
        mv = L["mv"][par]; mv_b = L["mv_b"][par]
        fw.op("dve", lambda h: h.scalar_tensor_tensor(mv[:rows, 2:3], mv[:rows, 0:1], -1.0, mv[:rows, 3:4], ALU.mult, ALU.mult), reads=[mv_b], writes=[mv_b])

    def ln_apply(t, par):
        L = ln_state
        rows = trows(t)
        mv = L["mv"][par]; mv_b = L["mv_b"][par]
        xb_ = xres_b[t]
        fw.op("act", lambda h: h.activation(xres[:rows, t, :], xres[:rows, t, :], AF.Identity, bias=mv[:rows, 2:3], scale=mv[:rows, 3:4]),
              reads=[xb_, mv_b], writes=[xb_])

    def ln_norm(t, par):
        ln_rstd(t, par); ln_nmr(t, par); ln_apply(t, par)

    def ln_gain(t):
        L = ln_state
        rows = trows(t)
        xb_ = xres_b[t]
        fw.op("dve", lambda h: h.tensor_tensor(xres[:rows, t, :], xres[:rows, t, :], L["gB"][:rows, :], ALU.mult), reads=[xb_, L["lnp_b"]], writes=[xb_])

    def ln_bias(t, use_pool=False):
        L = ln_state
        rows = trows(t)
        xb_ = xres_b[t]
        e2 = "pool" if use_pool else "dve"
        fw.op(e2, lambda h: h.tensor_tensor(xres[:rows, t, :], xres[:rows, t, :], L["bB"][:rows, :], ALU.add), reads=[xb_, L["lnp_b"]], writes=[xb_])

    def ln_affine(t, use_pool=False):
        ln_gain(t); ln_bias(t, use_pool)

    def layer_norm(t, par, use_pool=False):
        ln_stats(t, par); ln_norm(t, par); ln_affine(t, use_pool)

    ffn_pre = {}

    def ff1_block(hb, slot, fc, bi_, par):
        L = ln_state
        c0, n = BLOCKS[bi_]
        tl = block_tiles(bi_)
        bk = next_bank()
        w1t = wring[slot]

        def f1(h):
            ins = None
            for kc in range(8):
                ins = h.matmul(psb[bk][:, :n], w1t[:, kc, fc * 128:(fc + 1) * 128], actT[:, kc, c0:c0 + n], start=(kc == 0), stop=(kc == 7))
            return ins
        fw.op("pe", f1, reads=[actT_b[c][t] for c in range(8) for t in tl] + [wring_b[slot]], writes=[PB[bk]])
        fw.op("act", lambda h: h.activation(L["rl"][par][:, :n], psb[bk][:, :n], AF.Relu), reads=[PB[bk]], writes=[L["rl_b"][par]])
        fw.op("act", lambda h: h.activation(big2[:, hb * 4 + fc, c0:c0 + n], L["rl"][par][:, :n], AF.Square),
              reads=[L["rl_b"][par]], writes=[big2_b[hb * 4 + fc][t] for t in tl])

    def phase_C1(l):
        ln_setup()
        ln_load(ln1g_d[l], ln1b_d[l])
        s0 = acquire_piece(); s1 = acquire_piece(); sW1 = acquire_piece(); prefetch_pieces(3)
        slots = [s0, s1]
        ffn_pre["slot"] = sW1
        blk_last = {3: 0, 7: 1, 11: 2, 15: 3, 16: 4}
        xsrc = x_d if l == 0 else xsc_d
        items = []
        for t in range(NMT):
            def sL(t=t):
                rows = trows(t); c0 = tcol(t); par = t % 2
                rdx = [] if l == 0 else [xsc_b[t]]
                fw.dma("sp", lambda h, s: h.dma_start(out=xin[par][:rows, :], in_=xsrc[c0:c0 + rows, :]).then_inc(s, 16),
                       xin_b[par], "w", reads=rdx, writes=[xin_b[par]])

            def s0(t=t):
                rows = trows(t); c0 = tcol(t); par = t % 2
                for half in range(2):
                    bk = next_bank()
                    wt = wring[slots[half]]

                    def mm(h, bk=bk, wt=wt):
                        ins = None
                        for kc in range(8):
                            ins = h.matmul(psb[bk][:rows, :], actT[:, kc, c0:c0 + rows], wt[:, kc, :], start=(kc == 0), stop=(kc == 7))
                        return ins
                    fw.op("pe", mm, reads=[actT_b[c][tt] for c in range(8) for tt in cover(t)] + [wring_b[slots[half]]], writes=[PB[bk]])
                    fw.op("dve", lambda h, bk=bk, half=half: h.scalar_tensor_tensor(
                        xres[:rows, t, half * 512:(half + 1) * 512], xin[par][:rows, half * 512:(half + 1) * 512], ALPHA, psb[bk][:rows, :], ALU.mult, ALU.add),
                        reads=[PB[bk], xin_b[par]], writes=[xres_b[t]])
                ln_stats(t, t % 6)

            def s1(t=t):
                ln_rstd(t, t % 6)

            def s1b(t=t):
                ln_nmr(t, t % 6)

            def s1c(t=t):
                ln_apply(t, t % 6)

            def s2(t=t):
                ln_gain(t)

            def s2b(t=t):
                ln_bias(t, use_pool=True)

            def s2c(t=t):
                rows = trows(t)
                make_xT_front(t, xres[:rows, t, :], [xres_b[t]], t % 2)

            def s3(t=t):
                make_xT_back(t, t % 2, evac="act")
            def s4(t=t):
                if t in blk_last:
                    for fc in range(4):
                        ff1_block(0, sW1, fc, blk_last[t], fc % 2)
            items.append([sL, s0, s1, s1b, s1c, s2, s2b, s2c, s3, s4])
        run_multi(items)

    def phase_C2(l):
        L = ln_state
        ln_load(ln2g_d[l], ln2b_d[l])
        last = (l == DEPTH - 1)
        pend = []
        for g in range(8):
            if g == 0:
                s1 = ffn_pre["slot"]; s2 = acquire_piece()
            else:
                s1 = acquire_piece(); s2 = acquire_piece()
            prefetch_pieces(4 if g == 7 else 2)
            hb = g % 2
            w2v = wring[s2][:].rearrange("p a b -> p (a b)").rearrange("p (f n) -> p f n", f=4)
            if g > 0:
                cntr = 0
                for fc in range(4):
                    for bi_ in range(len(BLOCKS)):
                        ff1_block(hb, s1, fc, bi_, cntr % 2); cntr += 1
            pend.append((hb, w2v, s2))
            if g == 6:
                continue
            if g == 7:
                prefetch_pieces(3)
            srcs = list(pend)
            del pend[:]
            first_acc = (g == 0)
            fin = (g == 7)
            stages = []
            for t in range(NMT):
                def front(t=t, srcs=srcs, first_acc=first_acc, fin=fin):
                    rows = trows(t); c0 = tcol(t)
                    nmm = 4 * len(srcs)
                    for half in range(2):
                        bk = next_bank()

                        def f2(h, bk=bk, half=half):
                            ins = None
                            i = 0
                            for (hb_, w2v_, _s) in srcs:
                                for fc in range(4):
                                    ins = h.matmul(psb[bk][:rows, :], big2[:, hb_ * 4 + fc, c0:c0 + rows], w2v_[:, fc, half * 512:(half + 1) * 512], start=(i == 0), stop=(i == nmm - 1))
                                    i += 1
                            return ins
                        rds = []
                        for (hb_, w2v_, s2_) in srcs:
                            rds += [big2_b[hb_ * 4 + fc][tt] for fc in range(4) for tt in cover(t)] + [wring_b[s2_]]
                        fw.op("pe", f2, reads=rds, writes=[PB[bk]])
                        if first_acc:
                            fw.op("dve", lambda h, bk=bk, half=half: h.scalar_tensor_tensor(
                                xres[:rows, t, half * 512:(half + 1) * 512], xres[:rows, t, half * 512:(half + 1) * 512], ALPHA, psb[bk][:rows, :], ALU.mult, ALU.add),
                                reads=[PB[bk], xres_b[t]], writes=[xres_b[t]])
                        else:
                            fw.op("dve", lambda h, bk=bk, half=half: h.tensor_tensor(
                                xres[:rows, t, half * 512:(half + 1) * 512], xres[:rows, t, half * 512:(half + 1) * 512], psb[bk][:rows, :], ALU.add),
                                reads=[PB[bk], xres_b[t]], writes=[xres_b[t]])
                    if fin:
                        ln_stats(t, t % 6)

                def tl1(t=t):
                    ln_rstd(t, t % 6)

                def tl1b(t=t):
                    ln_nmr(t, t % 6)

                def tl1c(t=t):
                    ln_apply(t, t % 6)

                def tl1d(t=t):
                    ln_gain(t)

                def tl2(t=t):
                    rows = trows(t); c0 = tcol(t)
                    ln_bias(t, use_pool=True)
                    if last:
                        fw.dma("sp", lambda h, s: h.dma_start(out=y_d[c0:c0 + rows, :], in_=xres[:rows, t, :]).then_inc(s, 16),
                               xres_b[t], "r", reads=[xres_b[t]])
                    else:
                        fw.dma("sp", lambda h, s: h.dma_start(out=xsc_d[c0:c0 + rows, :], in_=xres[:rows, t, :]).then_inc(s, 16),
                               xres_b[t], "r", reads=[xres_b[t]], writes=[xsc_b[t]])
                        make_xT_front(t, xres[:rows, t, :], [xres_b[t]], t % 2)

                def tl3(t=t):
                    if not last:
                        make_xT_back(t, t % 2, evac="act")
                stages.append([front, tl1, tl1b, tl1c, tl1d, tl2, tl3] if fin else [front])
            run_multi(stages)
        prefetch_pieces(0)

    k_const()
    items0 = []
    for t in range(NMT):
        def p0(t=t):
            rows = trows(t); c0 = tcol(t); par = t % 2
            fw.dma("sp", lambda h, s: h.dma_start(out=xin[par][:rows, :], in_=x_d[c0:c0 + rows, :]).then_inc(s, 16),
                   xin_b[par], "w", writes=[xin_b[par]])
            make_xT_front(t, xin[par][:rows, :], [xin_b[par]], par)

        def p1(t=t):
            make_xT_back(t, t % 2, evac="dve")
        items0.append([p0, p1])
    run_multi(items0)
    for l in range(DEPTH):
        if STOP_AFTER == "X":
            break
        load_params(l)
        if STOP_AFTER == "P":
            break
        phase_A(l)
        fw.barrier()
        if STOP_AFTER is not None and STOP_AFTER.startswith("A%d" % l):
            break
        phase_B1(l)
        fw.barrier()
        if STOP_AFTER == "B1_%d" % l:
            break
        phase_B2(l)
        fw.barrier()
        if STOP_AFTER == "B2_%d" % l:
            break
        phase_C1(l)
        if STOP_AFTER == "C1_%d" % l:
            break
        phase_C2(l)
        fw.barrier()
        if STOP_AFTER == "C2_%d" % l:
            break

    fw.barrier()
    if DEBUG_DUMP:
        dbg_d = nc.dram_tensor("dbg", [128, 8 * TOK], BF16, kind="ExternalOutput").ap()
        dbgb = fw.buf("dbg")
        fw.dma("sp", lambda h, s: h.dma_start(out=dbg_d[:, :], in_=actT[:].rearrange("p a b -> p (a b)")).then_inc(s, 16), dbgb, "r", reads=[dbgb])
        fw.barrier()
    fw.emit()
    fw.close()
    return nc


_PROG = None


def _rope_tables():
    half = 32
    inv = 10000.0 ** (-np.arange(half, dtype=np.float64) / half)
    pos = np.zeros((NT, 128), np.float64)
    for t in range(NT):
        if t < 16:
            pos[t] = t * 128 + np.arange(128)
        else:
            pos[t, :64] = 2048 + np.arange(64)
            pos[t, 64:] = 2048 + np.arange(64)
    ang = pos[:, :, None] * inv[None, None, :]
    cos = np.cos(ang).astype(np.float32).transpose(1, 0, 2).copy()
    sin = np.sin(ang).astype(np.float32).transpose(1, 0, 2).copy()
    return cos, sin


def kernel(x_prompt, x_sample, cache_k, cache_v, cache_conv, w_in, lambda_q1, lambda_k1,
           lambda_q2, lambda_k2, subln_g, conv_w, conv_b, conv_ln_g, conv_ln_b, w_out,
           ln1_g, ln1_b, w_ff1, w_ff2, ln2_g, ln2_b):
    global _PROG
    f = lambda a: np.ascontiguousarray(np.asarray(a, dtype=np.float32))
    x_prompt = f(x_prompt); x_sample = f(x_sample)
    cache_k = f(cache_k); cache_v = f(cache_v); cache_conv = f(cache_conv)
    cos, sin = _rope_tables()
    shared = {
        "w_in": f(w_in), "w_out": f(w_out), "w_ff1": f(w_ff1), "w_ff2": f(w_ff2),
        "lq1": f(lambda_q1), "lk1": f(lambda_k1), "lq2": f(lambda_q2), "lk2": f(lambda_k2),
        "subln_g": f(subln_g), "conv_w": f(conv_w), "conv_b": f(conv_b),
        "conv_ln_g": f(conv_ln_g), "conv_ln_b": f(conv_ln_b),
        "ln1_g": f(ln1_g), "ln1_b": f(ln1_b), "ln2_g": f(ln2_g), "ln2_b": f(ln2_b),
        "rope_cos": cos, "rope_sin": sin,
    }
    in_maps = []
    for c in range(NCORE):
        m = dict(shared)
        m["x"] = np.ascontiguousarray(np.concatenate([x_prompt[c], x_sample[2 * c], x_sample[2 * c + 1]], axis=0))
        m["ck"] = np.ascontiguousarray(cache_k[:, 2 * c:2 * c + 2].reshape(DEPTH, 2, TP, 512))
        m["cv"] = np.ascontiguousarray(cache_v[:, 2 * c:2 * c + 2].reshape(DEPTH, 2, TP, 512))
        m["cc"] = np.ascontiguousarray(cache_conv[:, 2 * c:2 * c + 2])
        in_maps.append(m)
    if _PROG is None:
        _PROG = build_program()
    if DEBUG_CORES is not None:
        res = run_bass_kernel_spmd(_PROG, in_maps[:DEBUG_CORES], core_ids=list(range(DEBUG_CORES)))
        R = list(res.results) + [res.results[0]] * (NCORE - DEBUG_CORES)
        if DEBUG_DUMP:
            DEBUG_OUT["dbg"] = res.results[0]["dbg"]
    else:
        res = run_bass_kernel_spmd(_PROG, in_maps, core_ids=list(range(NCORE)))
        R = res.results
    y_prompt = np.stack([R[c]["y"][:TP] for c in range(NCORE)])
    y_sample = np.stack([R[c // 2]["y"][TP + (c % 2) * TS:TP + (c % 2 + 1) * TS] for c in range(2 * NCORE)])
    nk_p = np.stack([R[c]["nk"][:, :TP] for c in range(NCORE)], axis=1).reshape(DEPTH, NCORE, TP, 8, 64)
    nv_p = np.stack([R[c]["nv"][:, :TP] for c in range(NCORE)], axis=1).reshape(DEPTH, NCORE, TP, 4, 128)
    nc_p = np.stack([R[c]["ncv"][:, 0] for c in range(NCORE)], axis=1)
    nk_s = np.stack([R[c // 2]["nk"][:, TP + (c % 2) * TS:TP + (c % 2 + 1) * TS] for c in range(2 * NCORE)], axis=1).reshape(DEPTH, 2 * NCORE, TS, 8, 64)
    nv_s = np.stack([R[c // 2]["nv"][:, TP + (c % 2) * TS:TP + (c % 2 + 1) * TS] for c in range(2 * NCORE)], axis=1).reshape(DEPTH, 2 * NCORE, TS, 4, 128)
    nc_s = np.stack([R[c // 2]["ncv"][:, 1 + (c % 2)] for c in range(2 * NCORE)], axis=1)
    return (y_prompt, y_sample, np.ascontiguousarray(nk_p), np.ascontiguousarray(nv_p), np.ascontiguousarray(nc_p),
            np.ascontiguousarray(nk_s), np.ascontiguousarray(nv_s), np.ascontiguousarray(nc_s))
```
